# Optimizing a Trainium2 kernel written in Bass

```python
import jax, jax.numpy as jnp
from jax import lax
import numpy as np

D_MODEL = 1024
BATCH = 8
SEQ = 4096
DEPTH = 1

HEAD_DIM = 64
MIX_WIDTH = D_MODEL
ATTN_WIDTH = MIX_WIDTH // 2
SGU_WIDTH = MIX_WIDTH - ATTN_WIDTH
N_Q_HEADS = ATTN_WIDTH // HEAD_DIM
N_KV_HEADS = 2
Q_PER_KV = N_Q_HEADS // N_KV_HEADS
KV_WIDTH = N_KV_HEADS * HEAD_DIM
N_SGU_HEADS = 8
SGU_HEAD_DIM = SGU_WIDTH // N_SGU_HEADS
WINDOW = 128
BLOCK = 128
CHUNK = 128
NORM_EPS = 1e-5
NEG_INF = -1e30
SPLIT_SIZES = (ATTN_WIDTH, KV_WIDTH, KV_WIDTH, ATTN_WIDTH, SGU_WIDTH, SGU_WIDTH, SGU_WIDTH)
IN_WIDTH = sum(SPLIT_SIZES)

kernel_name = "hybrid_swa_sink_gmlp_parallel_heads"


def rmsnorm(x, g):
    xf = x.astype(jnp.float32)
    y = xf * lax.rsqrt(jnp.mean(xf * xf, axis=-1, keepdims=True) + NORM_EPS)
    return (y * g.astype(jnp.float32)).astype(x.dtype)


def layernorm(x, g, b):
    xf = x.astype(jnp.float32)
    mu = jnp.mean(xf, axis=-1, keepdims=True)
    xc = xf - mu
    y = xc * lax.rsqrt(jnp.mean(xc * xc, axis=-1, keepdims=True) + NORM_EPS)
    return (y * g.astype(jnp.float32) + b.astype(jnp.float32)).astype(x.dtype)


def banded_sink_attention(q, k, v, sinks):
    B, S = q.shape[0], q.shape[1]
    nb = S // BLOCK
    qb = q.reshape(B, nb, BLOCK, N_KV_HEADS, Q_PER_KV, HEAD_DIM)

    def band(t):
        tb = t.reshape(B, nb, BLOCK, N_KV_HEADS, HEAD_DIM)
        prev = jnp.pad(tb, ((0, 0), (1, 0), (0, 0), (0, 0), (0, 0)))[:, :-1]
        return jnp.concatenate([prev, tb], axis=2)

    kb, vb = band(k), band(v)
    scale = HEAD_DIM ** -0.5
    scores = jnp.einsum('bnqhgd,bnkhd->bnhgqk', qb, kb).astype(jnp.float32) * scale
    qi = jnp.arange(BLOCK)[:, None] + BLOCK
    kj = jnp.arange(2 * BLOCK)[None, :]
    diff = qi - kj
    in_window = (diff >= 0) & (diff < WINDOW)
    key_pos = jnp.arange(nb)[:, None, None] * BLOCK - BLOCK + kj[None]
    valid = in_window[None] & (key_pos >= 0)
    scores = jnp.where(valid[None, :, None, None], scores, NEG_INF)
    sink = sinks.astype(jnp.float32).reshape(N_KV_HEADS, Q_PER_KV)[None, None, :, :, None, None]
    m = jnp.maximum(jnp.max(scores, axis=-1, keepdims=True), sink)
    p = jnp.exp(scores - m)
    probs = p / (jnp.sum(p, axis=-1, keepdims=True) + jnp.exp(sink - m))
    out = jnp.einsum('bnhgqk,bnkhd->bnqhgd', probs.astype(vb.dtype), vb)
    return out.reshape(B, S, ATTN_WIDTH)


def chunked_spatial_gating(u, v, w_s, b_s, ln_g, ln_b):
    B, S = u.shape[0], u.shape[1]
    nc = S // CHUNK
    v = layernorm(v, ln_g, ln_b)
    vc = v.reshape(B, nc, CHUNK, N_SGU_HEADS, SGU_HEAD_DIM)
    causal = jnp.tril(jnp.ones((CHUNK, CHUNK), dtype=bool))
    w = jnp.where(causal[None], w_s, jnp.zeros_like(w_s)).astype(vc.dtype)
    mixed = jnp.einsum('hts,bcshd->bcthd', w, vc) + b_s.T.astype(vc.dtype)[None, None, :, :, None]
    return u * mixed.reshape(B, S, SGU_WIDTH)


def setup_inputs(seed: int = 0) -> dict:
    key = jax.random.key(seed)
    ks = jax.random.split(key, 12)
    f32 = jnp.float32
    x = jax.random.normal(ks[0], (BATCH, SEQ, D_MODEL), f32)
    norm_g = 1.0 + 0.02 * jax.random.normal(ks[1], (DEPTH, D_MODEL), f32)
    w_in = jax.random.normal(ks[2], (DEPTH, D_MODEL, IN_WIDTH), f32) * D_MODEL ** -0.5
    b_in = 0.02 * jax.random.normal(ks[3], (DEPTH, IN_WIDTH), f32)
    attn_sinks = 0.5 * jax.random.normal(ks[4], (DEPTH, N_Q_HEADS), f32)
    sgu_ln_g = 1.0 + 0.02 * jax.random.normal(ks[5], (DEPTH, SGU_WIDTH), f32)
    sgu_ln_b = 0.02 * jax.random.normal(ks[6], (DEPTH, SGU_WIDTH), f32)
    sgu_w = jax.random.normal(ks[7], (DEPTH, N_SGU_HEADS, CHUNK, CHUNK), f32) * CHUNK ** -0.5
    sgu_b = 1.0 + 0.02 * jax.random.normal(ks[8], (DEPTH, N_SGU_HEADS, CHUNK), f32)
    w_out = jax.random.normal(ks[9], (DEPTH, MIX_WIDTH, D_MODEL), f32) * MIX_WIDTH ** -0.5
    b_out = 0.02 * jax.random.normal(ks[10], (DEPTH, D_MODEL), f32)
    final_norm_g = 1.0 + 0.02 * jax.random.normal(ks[11], (D_MODEL,), f32)
    return {"x": x, "norm_g": norm_g, "w_in": w_in, "b_in": b_in,
            "attn_sinks": attn_sinks, "sgu_ln_g": sgu_ln_g, "sgu_ln_b": sgu_ln_b,
            "sgu_w": sgu_w, "sgu_b": sgu_b, "w_out": w_out, "b_out": b_out,
            "final_norm_g": final_norm_g}


def reference(x, norm_g, w_in, b_in, attn_sinks, sgu_ln_g, sgu_ln_b, sgu_w, sgu_b,
              w_out, b_out, final_norm_g):
    split_points = list(np.cumsum(SPLIT_SIZES)[:-1])
    for l in range(DEPTH):
        h = rmsnorm(x, norm_g[l])
        proj = jnp.einsum('bsd,de->bse', h, w_in[l]) + b_in[l]
        q, k, v, z_a, u_s, v_s, z_s = jnp.split(proj, split_points, axis=-1)
        attn = banded_sink_attention(q, k, v, attn_sinks[l]) * jax.nn.silu(z_a)
        u_s = jax.nn.gelu(u_s, approximate=False)
        v_s = jax.nn.gelu(v_s, approximate=False)
        sgu = chunked_spatial_gating(u_s, v_s, sgu_w[l], sgu_b[l], sgu_ln_g[l], sgu_ln_b[l]) * jax.nn.silu(z_s)
        mixed = jnp.concatenate([attn, sgu], axis=-1)
        x = x + jnp.einsum('bse,ed->bsd', mixed, w_out[l]) + b_out[l]
    return rmsnorm(x, final_norm_g)
```

```python
from contextlib import ExitStack

import numpy as np
import ml_dtypes
import concourse.bass as bass
import concourse.mybir as mybir
from concourse.bass_utils import run_bass_kernel_spmd

F32 = mybir.dt.float32
BF16 = mybir.dt.bfloat16
AF = mybir.ActivationFunctionType
ALU = mybir.AluOpType

D = 1024
T = 4096
NCORES = 8
NT = T // 128
NS = T // 512
INW = 2816
EPS = 1e-5
NX = 8
OFF_Q, OFF_K, OFF_V, OFF_U, OFF_VS, OFF_ZA, OFF_ZS = 0, 512, 640, 768, 1280, 1792, 2304
PIECES = [(0, 768), (768, 1792), (1792, 2816)]
FM_COLS = ([(OFF_Q + 128 * c, 0) for c in range(4)] + [(OFF_K, 0)] + [(OFF_U + 128 * c, 1) for c in range(4)]
           + [(OFF_ZA + 128 * c, 2) for c in range(4)] + [(OFF_ZS + 128 * c, 2) for c in range(4)])
NFM = len(FM_COLS)


class Sched:
    ENGS = ("pe", "act", "dve", "pool", "sp")

    def __init__(self, eng_sems):
        self.prog = {e: [] for e in self.ENGS}
        self.eng_sem = eng_sems
        self.cnt = {}
        self.seen = {e: {} for e in self.ENGS}
        self.lw = {}
        self.rd = {}

    def op(self, eng, fn, reads=(), writes=(), dsem=None):
        evs = []
        for k in reads:
            if k in self.lw:
                evs.append(self.lw[k])
        for k in writes:
            if k in self.lw:
                evs.append(self.lw[k])
            evs.extend(self.rd.get(k, ()))
        need = {}
        for (sem, val, e2) in evs:
            if e2 == "pe" and eng == "pe":
                continue
            k = id(sem)
            if self.seen[eng].get(k, 0) >= val:
                continue
            if need.get(k, (None, 0))[1] < val:
                need[k] = (sem, val)
        waits = list(need.values())
        for (sem, val) in waits:
            self.seen[eng][id(sem)] = val
        sem, inc = (self.eng_sem[eng], 1) if dsem is None else (dsem, 16)
        self.cnt[id(sem)] = self.cnt.get(id(sem), 0) + inc
        ev = (sem, self.cnt[id(sem)], eng if dsem is None else "dma")
        self.prog[eng].append((fn, waits, (sem, inc)))
        for k in reads:
            self.rd.setdefault(k, []).append(ev)
        for k in writes:
            self.lw[k] = ev
            self.rd[k] = []
        return ev

    def regroup(self, keys):
        last = max((self.lw[k] for k in keys), key=lambda ev: ev[1])
        for k in keys:
            self.lw[k] = last

    def final_wait(self, eng, keys):
        waits = {}
        for k in keys:
            sem, val, _ = self.lw[k]
            if waits.get(id(sem), (None, 0))[1] < val:
                waits[id(sem)] = (sem, val)
        self.prog[eng].append((None, list(waits.values()), None))

    def emit(self, block):
        def run(engine, items):
            for fn, waits, inc in items:
                for (sem, val) in waits:
                    engine.wait_ge(sem, val)
                if fn is None:
                    continue
                fn(engine).then_inc(inc[0], inc[1])

        block.tensor(lambda e: run(e, self.prog["pe"]))
        block.scalar(lambda e: run(e, self.prog["act"]))
        block.vector(lambda e: run(e, self.prog["dve"]))
        block.gpsimd(lambda e: run(e, self.prog["pool"]))
        block.sync(lambda e: run(e, self.prog["sp"]))


def build_program():
    nc = bass.Bass("TRN2", target_bir_lowering=False)

    def din(name, shape, dt=F32):
        return nc.dram_tensor(name, list(shape), dt, kind="ExternalInput").ap()

    x_d = din("x", [T, D])
    win_d = din("w_in", [D, INW])
    wout_d = din("w_out", [D, D])
    gcol_d = din("gcol", [128, 8])
    biasfm_d = din("bias_fm", [128, NFM])
    bv_d = din("bv_bc", [128, 128])
    bvs_d = din("bvs_row", [1, 512])
    sink_d = din("sinks", [128, 4])
    wT_d = din("sgu_wT", [128, 8, 128])
    tril_d = din("tril", [128, 128])
    bsB_d = din("bsB", [128, 4, 128])
    lnb_d = din("lnb_col", [128, 4])
    lng_d = din("lng_col", [128, 4])
    bout_d = din("bout_bc", [128, D])
    gf_d = din("gf_bc", [128, D])
    ident_d = din("ident", [128, 128], BF16)
    mask_d = din("mask", [128, 2, 512], BF16)
    out_d = nc.dram_tensor("out", [T, D], F32, kind="ExternalOutput").ap()

    with ExitStack() as es:
        def sb(name, shape, dt):
            return es.enter_context(nc.sbuf_tensor(name, list(shape), dt))

        def ps(name, shape, dt):
            return es.enter_context(nc.psum_tensor(name, list(shape), dt))

        def sem(name):
            return es.enter_context(nc.semaphore(name))

        w_in = sb("w_in_bf", [128, 8, INW], BF16)
        w_out = sb("w_out_bf", [128, 8, D], BF16)
        x_t = sb("x_t", [128, NX, D], F32)
        xn = sb("xn_bf", [128, 4, D], BF16)
        xT = sb("xT", [128, 8, 512], BF16)
        qT = sb("qT", [128, 2, 4, 512], BF16)
        kT = sb("kT", [128, 8, 128], BF16)
        v_r = sb("v_r", [128, 8, 128], BF16)
        zaT = sb("zaT", [128, 2, 4, 512], BF16)
        uT = sb("uT", [128, 4, 512], BF16)
        zsT = sb("zsT", [128, 4, 512], BF16)
        vs_f = sb("vs_f", [128, 2, 512], F32)
        vln = sb("vln", [128, 4, 512], BF16)
        PT = sb("PT", [128, 2, 2, 2, 512], BF16)
        LR = sb("LR", [128, 2, 512], F32)
        mixT = sb("mixT", [128, 8, 512], BF16)
        sgt = sb("sgt", [128, 4, 512], F32)
        gate = sb("gate", [128, 4, 512], BF16)
        y_t = sb("y_t", [128, 2, D], F32)
        junk = sb("junk", [128, D], BF16)
        ident = sb("ident_sb", [128, 128], BF16)
        mask = sb("mask_sb", [128, 2, 512], BF16)
        ones64 = sb("ones64", [128, 64], BF16)
        ones_row = sb("ones_row", [1, 128], BF16)
        esink_col = sb("esink_col", [128, 4], F32)
        gcol = sb("gcol_sb", [128, 8], F32)
        bias_fm = sb("bias_fm_sb", [128, NFM], F32)
        bv_bc = sb("bv_bc_sb", [128, 128], F32)
        bvs_f = sb("bvs_f", [1, 512], F32)
        bvs_row = sb("bvs_row_sb", [1, 512], BF16)
        WT = sb("WT_bf", [128, 8, 128], BF16)
        tril = sb("tril_sb", [128, 128], F32)
        bsB = sb("bsB_sb", [128, 4, 128], F32)
        lnb = sb("lnb_sb", [128, 4], F32)
        lng = sb("lng_sb", [128, 4], F32)
        Cmat = sb("Cmat", [128, 4, 128], F32)
        bout = sb("bout_sb", [128, D], F32)
        gf = sb("gf_sb", [128, D], F32)
        nh = sb("neghalf", [128, 1], F32)
        eps_col = sb("eps_col", [128, 1], F32)
        dmy = sb("dmy", [128, 2], F32)
        st_ss = sb("st_ss", [128, 4], F32)
        st_ms = sb("st_ms", [128, 4], F32)
        st_rs = sb("st_rs", [128, 4], F32)
        st_bn = sb("st_bn", [128, 2, 6], F32)
        st_mv = sb("st_mv", [128, 2, 2], F32)
        st_ve = sb("st_ve", [128, 2], F32)
        st_lr = sb("st_lr", [128, 2], F32)
        st_fs = sb("st_fs", [128, 2], F32)
        st_fm = sb("st_fm", [128, 2], F32)
        st_fr = sb("st_fr", [128, 2], F32)
        NB = 8
        pbank = [ps(f"pb{i}", [128, 512], F32) for i in range(NB)]
        esems = {e: sem("s_" + e) for e in ("pe", "act", "dve", "pool")}
        d_const = sem("d_const")
        d_x = [sem(f"d_x{i}") for i in range(NX)]
        d_wo = sem("d_wo")
        d_wt = sem("d_wt")
        d_const2 = sem("d_const2")
        d_y = [sem(f"d_y{i}") for i in range(2)]

        S = Sched(esems)
        bank_ctr = [0]

        def newbank():
            b = bank_ctr[0] % NB
            bank_ctr[0] += 1
            return b

        consts = [
            ("c_gcol", gcol[:], gcol_d), ("c_biasfm", bias_fm[:], biasfm_d), ("c_bv", bv_bc[:], bv_d),
            ("c_bvs", bvs_f[:], bvs_d), ("c_sink", esink_col[:], sink_d), ("c_tril", tril[:], tril_d),
            ("c_bsB", bsB[:], bsB_d), ("c_lnb", lnb[:], lnb_d), ("c_lng", lng[:], lng_d),
            ("c_ident", ident[:], ident_d), ("c_mask", mask[:], mask_d),
        ]
        consts_late = [("c_bout", bout[:], bout_d), ("c_gf", gf[:], gf_d)]

        def load_x(n):
            slot = n % NX
            S.op("sp", lambda e: e.dma_start(out=x_t[:, slot, :], in_=x_d[n * 128:(n + 1) * 128, :]),
                 writes=[("x", slot)], dsem=d_x[slot])

        for n in range(4):
            load_x(n)
        for key, dst, src in consts:
            S.op("sp", lambda e, dst=dst, src=src: e.dma_start(out=dst, in_=src), writes=[key], dsem=d_const)
        S.regroup([c[0] for c in consts])
        wt_stage = sgt[:, 0:2, :].rearrange("p a (h t) -> p (a h) t", t=128)
        S.op("sp", lambda e: e.dma_start(out=wt_stage, in_=wT_d), writes=[("sgt", 0), ("sgt", 1)], dsem=d_wt)

        def load_consts_late():
            for key, dst, src in consts_late:
                S.op("sp", lambda e, dst=dst, src=src: e.dma_start(out=dst, in_=src), writes=[key], dsem=d_const2)
            S.regroup([c[0] for c in consts_late])

        S.op("pool", lambda e: e.memset(nh[:], -0.5), writes=["nh"])
        S.op("pool", lambda e: e.memset(eps_col[:], EPS), writes=["eps_col"])
        S.op("pool", lambda e: e.memset(dmy[:], 1.0), writes=["dmy"])

        def warm(func):
            S.op("act", lambda e: e.activation(out=dmy[:, 1:2], in_=dmy[:, 0:1], func=func), reads=["dmy"], writes=["dmy1"])
        S.op("pool", lambda e: e.memset(ones64[:], 1.0), writes=["ones64"])
        S.op("pool", lambda e: e.memset(ones_row[:], 1.0), writes=["ones_row"])

        win_v = win_d.rearrange("(k p) c -> k p c", p=128)
        conv_eng = ["pool", "pool", "pool", "pool"]
        stage_i = [0]

        def stage_piece(pc):
            c0, c1 = PIECES[pc]
            for k in range(8):
                i = stage_i[0]
                stage_i[0] += 1
                si = i % 6
                if si < 4:
                    skey, ssem, src = ("x", 4 + si), d_x[4 + si], x_t[:, 4 + si, 0:c1 - c0]
                else:
                    skey, ssem, src = ("y", si - 4), d_y[si - 4], y_t[:, si - 4, 0:c1 - c0]
                S.op("sp", lambda e, k=k, src=src: e.dma_start(out=src, in_=win_v[k, :, c0:c1]),
                     writes=[skey], dsem=ssem)
                ce = conv_eng[i % 4]
                dst = w_in[:, k, c0:c1]
                if ce == "dve":
                    fn = lambda e, dst=dst, src=src, k=k: e.tensor_scalar(out=dst, in0=src, scalar1=gcol[:, k:k + 1], scalar2=None, op0=ALU.mult)
                elif ce == "act":
                    fn = lambda e, dst=dst, src=src, k=k: e.activation(out=dst, in_=src, func=AF.Copy, scale=gcol[:, k:k + 1])
                else:
                    fn = lambda e, dst=dst, src=src, k=k: e.tensor_scalar(out=dst, in0=src, scalar1=gcol[:, k:k + 1], scalar2=1.0, op0=ALU.mult, op1=ALU.mult)
                S.op(ce, fn, reads=[skey, "c_gcol"], writes=[("w_in", k, pc)])

        def load_w_out():
            wout_v = wout_d.rearrange("(k p) c -> k p c", p=128)
            for k in range(8):
                S.op("pool", lambda e, k=k: e.dma_start(out=w_out[:, k, :], in_=wout_v[k]), writes=[("w_out", k)], dsem=d_wo)
            S.regroup([("w_out", k) for k in range(8)])

        def setup_late():
            S.op("dve", lambda e: e.tensor_tensor(out=WT[:], in0=wt_stage, in1=tril[:].unsqueeze(1).to_broadcast([128, 8, 128]), op=ALU.mult),
                 reads=[("sgt", 0), ("sgt", 1), "c_tril"], writes=["WT"])
            S.op("act", lambda e: e.activation(out=esink_col[:], in_=esink_col[:], func=AF.Exp), reads=["c_sink"], writes=["esink_col"])
            S.op("dve", lambda e: e.tensor_copy(out=bvs_row[:], in_=bvs_f[:]), reads=["c_bvs"], writes=["bvs_row"])
            cb = newbank()

            def cmat_mm(e):
                ins = None
                for jj in range(4):
                    for hh in range(2):
                        ins = e.matmul(pbank[cb][hh * 64:(hh + 1) * 64, jj * 128:(jj + 1) * 128], lhsT=ones64[:, :],
                                       rhs=WT[:, 2 * jj + hh, :], start=True, stop=True)
                return ins
            S.op("pe", cmat_mm, reads=["ones64", "WT"], writes=[("pb", cb)])
            for jj in range(4):
                S.op("dve", lambda e, jj=jj: e.scalar_tensor_tensor(out=Cmat[:, jj, :], in0=pbank[cb][:, jj * 128:(jj + 1) * 128],
                                                                    scalar=lnb[:, jj:jj + 1], in1=bsB[:, jj, :], op0=ALU.mult, op1=ALU.add),
                     reads=[("pb", cb), "c_lnb", "c_bsB"], writes=[("Cmat", jj)])

        def wkeys(pc):
            return [("w_in", k, pc) for k in range(8)]

        def A_pre_units(st, use_pool=False, part=0):
            def mk(tt):
                def unit():
                    n = 4 * st + tt
                    slot = n % NX
                    xs = x_t[:, slot, :]
                    if part != 2:
                        S.op("act", lambda e: e.activation(out=junk[:], in_=xs, func=AF.Square, accum_out=st_ss[:, tt:tt + 1]),
                             reads=[("x", slot)], writes=["junk", ("ss", tt)])
                    if use_pool and part == 2:
                        S.op("act", lambda e: e.activation(out=xn[:, tt, :], in_=xs, func=AF.Copy, scale=st_rs[:, tt:tt + 1]),
                             reads=[("x", slot), ("rs", tt)], writes=[("xn", tt)])
                    elif use_pool:
                        S.op("pool", lambda e: e.tensor_scalar(out=st_ms[:, tt:tt + 1], in0=st_ss[:, tt:tt + 1], scalar1=1.0 / D, scalar2=EPS,
                                                               op0=ALU.mult, op1=ALU.add),
                             reads=[("ss", tt)], writes=[("ms", tt)])
                        S.op("pool", lambda e: e.tensor_tensor(out=st_rs[:, tt:tt + 1], in0=st_ms[:, tt:tt + 1], in1=nh[:], op=ALU.pow),
                             reads=[("ms", tt), "nh"], writes=[("rs", tt)])
                        if part == 0:
                            S.op("act", lambda e: e.activation(out=xn[:, tt, :], in_=xs, func=AF.Copy, scale=st_rs[:, tt:tt + 1]),
                                 reads=[("x", slot), ("rs", tt)], writes=[("xn", tt)])
                    else:
                        S.op("dve", lambda e: e.tensor_scalar(out=st_ms[:, tt:tt + 1], in0=st_ss[:, tt:tt + 1], scalar1=1.0 / D, scalar2=EPS,
                                                              op0=ALU.mult, op1=ALU.add),
                             reads=[("ss", tt)], writes=[("ms", tt)])
                        S.op("act", lambda e: e.activation(out=st_rs[:, tt:tt + 1], in_=st_ms[:, tt:tt + 1], func=AF.Ln),
                             reads=[("ms", tt)], writes=[("rs", tt)])
                        S.op("act", lambda e: e.activation(out=st_rs[:, tt:tt + 1], in_=st_rs[:, tt:tt + 1], func=AF.Exp, scale=-0.5),
                             reads=[("rs", tt)], writes=[("rs", tt)])
                        S.op("dve", lambda e: e.tensor_scalar(out=xn[:, tt, :], in0=xs, scalar1=st_rs[:, tt:tt + 1], scalar2=None, op0=ALU.mult),
                             reads=[("x", slot), ("rs", tt)], writes=[("xn", tt)])
                return unit
            return [mk(tt) for tt in range(4)]

        def XB_units(st):
            def mk(tt):
                def unit():
                    slot = (4 * st + tt) % NX
                    xs = x_t[:, slot, :]
                    S.op("pool", lambda e: e.tensor_tensor(out=xs, in0=xs, in1=bout[:], op=ALU.add),
                         reads=[("x", slot), "c_bout"], writes=[("x", slot)])
                return unit
            return [mk(tt) for tt in range(4)]

        def A_pe_units(st):
            def mk(pair):
                def unit():
                    tts = (2 * pair, 2 * pair + 1)
                    banks = [newbank(), newbank()]
                    pts = [pbank[b][:].bitcast(BF16).rearrange("p (c t) -> p c t", c=8) for b in banks]

                    def tr(e):
                        ins = None
                        for i, tt in enumerate(tts):
                            for c in range(8):
                                ins = e.transpose(pts[i][:, c, :], xn[:, tt, c * 128:(c + 1) * 128], ident[:])
                        return ins
                    S.op("pe", tr, reads=[("xn", tt) for tt in tts] + ["c_ident"], writes=[("pb", b) for b in banks])
                    for i, tt in enumerate(tts):
                        S.op("dve", lambda e, i=i, tt=tt: e.tensor_copy(out=xT[:, :, tt * 128:(tt + 1) * 128], in_=pts[i]),
                             reads=[("pb", banks[i])], writes=[("xT", tt)])
                return unit
            return [mk(0), mk(1)]

        xT_keys = [("xT", tt) for tt in range(4)]

        def fm_unit(fis, func, dsts, keys):
            def unit():
                banks = [newbank() for _ in fis]

                def mm(e):
                    ins = None
                    for fi, b in zip(fis, banks):
                        col = FM_COLS[fi][0]
                        for k in range(8):
                            ins = e.matmul(pbank[b][:], lhsT=w_in[:, k, col:col + 128], rhs=xT[:, k, :], start=(k == 0), stop=(k == 7))
                    return ins
                pcs = sorted({FM_COLS[fi][1] for fi in fis})
                S.op("pe", mm, reads=xT_keys + [kk for pc in pcs for kk in wkeys(pc)], writes=[("pb", b) for b in banks])
                for fi, b, dst, key in zip(fis, banks, dsts, keys):
                    if func == AF.Identity:
                        S.op("dve", lambda e, fi=fi, b=b, dst=dst: e.tensor_scalar(out=dst, in0=pbank[b][:], scalar1=bias_fm[:, fi:fi + 1],
                                                                                   scalar2=None, op0=ALU.add),
                             reads=[("pb", b), "c_biasfm"], writes=[key])
                    else:
                        S.op("act", lambda e, fi=fi, b=b, dst=dst: e.activation(out=dst, in_=pbank[b][:], func=func, bias=bias_fm[:, fi:fi + 1]),
                             reads=[("pb", b), "c_biasfm"], writes=[key])
            return unit

        def BQKV_units(st):
            qb = st % 2
            us = [fm_unit([c, c + 1], AF.Identity, [qT[:, qb, c, :], qT[:, qb, c + 1, :]], [("qT", qb, c), ("qT", qb, c + 1)]) for c in (0, 2)]
            kslot = (st % 2) * 4
            us.append(fm_unit([4], AF.Identity, [kT[:, kslot:kslot + 4, :].rearrange("p a b -> p (a b)")], [("kT", st % 2)]))

            def vunit():
                b = newbank()
                vslot = (st % 2) * 4

                def mmv(e):
                    ins = None
                    for tt in range(4):
                        for k in range(8):
                            ins = e.matmul(pbank[b][:, tt * 128:(tt + 1) * 128], lhsT=xT[:, k, tt * 128:(tt + 1) * 128],
                                           rhs=w_in[:, k, OFF_V:OFF_V + 128], start=(k == 0), stop=(k == 7))
                    return ins
                S.op("pe", mmv, reads=xT_keys + wkeys(0), writes=[("pb", b)])
                S.op("dve", lambda e: e.tensor_tensor(out=v_r[:, vslot:vslot + 4, :], in0=pbank[b][:].rearrange("p (a c) -> p a c", a=4),
                                                      in1=bv_bc[:].unsqueeze(1).to_broadcast([128, 4, 128]), op=ALU.add),
                     reads=[("pb", b), "c_bv"], writes=[("v", vslot + i) for i in range(4)])
            us.append(vunit)
            return us

        def Brest_units(st):
            zb = st % 2
            us = [fm_unit([5 + c, 6 + c], AF.Gelu, [uT[:, c, :], uT[:, c + 1, :]], [("uT", c), ("uT", c + 1)]) for c in (0, 2)]

            def mkvs(tt):
                def unit():
                    b = newbank()
                    vb = tt % 2

                    def mmvs(e):
                        for k in range(8):
                            e.matmul(pbank[b][:], lhsT=xT[:, k, tt * 128:(tt + 1) * 128], rhs=w_in[:, k, OFF_VS:OFF_VS + 512],
                                     start=(k == 0), stop=False)
                        return e.matmul(pbank[b][:], lhsT=ones_row[0:1, :], rhs=bvs_row[0:1, :], start=False, stop=True)
                    S.op("pe", mmvs, reads=[("xT", tt), "ones_row", "bvs_row"] + wkeys(1), writes=[("pb", b)])
                    S.op("act", lambda e: e.activation(out=vs_f[:, vb, :], in_=pbank[b][:], func=AF.Gelu),
                         reads=[("pb", b)], writes=[("vs_f", vb)])
                    S.op("dve", lambda e: e.bn_stats(out=st_bn[:, vb, :], in_=vs_f[:, vb, :]), reads=[("vs_f", vb)], writes=[("bn", vb)])
                    S.op("dve", lambda e: e.bn_aggr(out=st_mv[:, vb, :], in_=st_bn[:, vb, :]), reads=[("bn", vb)], writes=[("mv", vb)])
                    S.op("dve", lambda e: e.tensor_scalar(out=st_ve[:, vb:vb + 1], in0=st_mv[:, vb, 1:2], scalar1=EPS, scalar2=None, op0=ALU.add),
                         reads=[("mv", vb)], writes=[("ve", vb)])
                    S.op("pool", lambda e: e.tensor_tensor(out=st_lr[:, vb:vb + 1], in0=st_ve[:, vb:vb + 1], in1=nh[:], op=ALU.pow),
                         reads=[("ve", vb), "nh"], writes=[("lr", vb)])
                    S.op("dve", lambda e: e.tensor_scalar(out=vln[:, tt, :], in0=vs_f[:, vb, :], scalar1=st_mv[:, vb, 0:1],
                                                          scalar2=st_lr[:, vb:vb + 1], op0=ALU.subtract, op1=ALU.mult),
                         reads=[("vs_f", vb), ("mv", vb), ("lr", vb)], writes=[("vln", tt)])
                return unit
            us += [mkvs(tt) for tt in range(4)]
            us += [fm_unit([9 + c, 10 + c], AF.Silu, [zaT[:, zb, c, :], zaT[:, zb, c + 1, :]], [("zaT", zb, c), ("zaT", zb, c + 1)])
                   for c in (0, 2)]
            us += [fm_unit([13 + c, 14 + c], AF.Silu, [zsT[:, c, :], zsT[:, c + 1, :]], [("zsT", c), ("zsT", c + 1)]) for c in (0, 2)]
            return us

        cstate = {}

        def C1_unit(st, j):
            def unit():
                n = 4 * st + j
                qb = st % 2
                kbs = [1] if n == 0 else [0, 1]
                qkeys = [("qT", qb, c) for c in range(4)]
                sbk = {}
                order = [(g, kb) for kb in kbs for g in range(2)]
                for gk in order:
                    sbk[gk] = newbank()

                def mms(e):
                    ins = None
                    for (g, kb) in order:
                        kslot = (n - 1 + kb) % 8
                        ins = e.matmul(pbank[sbk[(g, kb)]][:], lhsT=kT[g * 64:(g + 1) * 64, kslot, :],
                                       rhs=qT[g * 64:(g + 1) * 64, qb, :, j * 128:(j + 1) * 128], start=True, stop=True)
                    return ins
                S.op("pe", mms, reads=qkeys + [("kT", ((n - 1 + kb) // 4) % 2) for kb in kbs], writes=[("pb", b) for b in sbk.values()])
                pp = n % 2
                for kb in kbs:
                    for g in range(2):
                        b = sbk[(g, kb)]
                        S.op("act", lambda e, g=g, kb=kb, b=b: e.activation(out=PT[:, pp, kb, g, :], in_=pbank[b][:], func=AF.Exp, scale=0.125),
                             reads=[("pb", b)], writes=[("PT", pp, kb, g)])
                        def msk(e, g=g, kb=kb):
                            v = PT[:, pp, kb, g, :].rearrange("p (c q) -> p c q", c=4)
                            if kb == 1:
                                return e.affine_select(out=v, in_=v, pattern=[[0, 4], [1, 128]], compare_op=ALU.is_ge, fill=0.0,
                                                       base=0, channel_multiplier=-1)
                            return e.affine_select(out=v, in_=v, pattern=[[0, 4], [-1, 128]], compare_op=ALU.is_ge, fill=0.0,
                                                   base=-1, channel_multiplier=1)
                        if kb == 0:
                            S.op("pool", msk, reads=[("PT", pp, kb, g)], writes=[("PT", pp, kb, g)])
                        else:
                            S.op("dve", lambda e, g=g, kb=kb: e.tensor_tensor(out=PT[:, pp, kb, g, :], in0=PT[:, pp, kb, g, :],
                                                                              in1=mask[:, kb, :], op=ALU.mult),
                                 reads=[("PT", pp, kb, g), "c_mask"], writes=[("PT", pp, kb, g)])
            return unit

        def C2_unit(st, j):
            def unit():
                n = 4 * st + j
                zb = st % 2
                pp = n % 2
                kbs = [1] if n == 0 else [0, 1]
                bn_ = newbank()
                bd_ = newbank()

                def mmpv(e):
                    for g in range(2):
                        for i, kb in enumerate(kbs):
                            kn = n - 1 + kb
                            e.matmul(pbank[bn_][g * 64:(g + 1) * 64, :], lhsT=v_r[:, kn % 8, g * 64:(g + 1) * 64], rhs=PT[:, pp, kb, g, :],
                                     start=(i == 0), stop=(i == len(kbs) - 1))
                    ins = None
                    for g in range(2):
                        for i, kb in enumerate(kbs):
                            ins = e.matmul(pbank[bd_][g * 64:(g + 1) * 64, :], lhsT=ones64[:, :], rhs=PT[:, pp, kb, g, :],
                                           start=(i == 0), stop=(i == len(kbs) - 1))
                    return ins
                vkeys = [("v", (n - 1 + kb) % 8) for kb in kbs]
                S.op("pe", mmpv, reads=[("PT", pp, kb, g) for kb in kbs for g in range(2)] + vkeys + ["ones64"],
                     writes=[("pb", bn_), ("pb", bd_)])
                lb = j % 2
                def lnop(e):
                    ins = None
                    for c in range(4):
                        ins = e.activation(out=LR[:, lb, c * 128:(c + 1) * 128], in_=pbank[bd_][:, c * 128:(c + 1) * 128], func=AF.Ln,
                                           bias=esink_col[:, c:c + 1])
                    return ins
                S.op("act", lnop, reads=[("pb", bd_), "esink_col"], writes=[("LR", lb)])
                S.op("act", lambda e: e.activation(out=LR[:, lb, :], in_=LR[:, lb, :], func=AF.Exp, scale=-1.0), reads=[("LR", lb)], writes=[("LR", lb)])
                S.op("pool", lambda e: e.tensor_tensor(out=LR[:, lb, :].rearrange("p (c q) -> p c q", c=4),
                                                       in0=LR[:, lb, :].rearrange("p (c q) -> p c q", c=4),
                                                       in1=zaT[:, zb, :, j * 128:(j + 1) * 128], op=ALU.mult),
                     reads=[("LR", lb)] + [("zaT", zb, c) for c in range(4)], writes=[("LR", lb)])
                S.op("dve", lambda e: e.tensor_tensor(out=mixT[:, 0:4, j * 128:(j + 1) * 128],
                                                      in0=pbank[bn_][:].rearrange("p (c q) -> p c q", c=4),
                                                      in1=LR[:, lb, :].rearrange("p (c q) -> p c q", c=4), op=ALU.mult),
                     reads=[("pb", bn_), ("LR", lb)], writes=[("mixA", j)])
            return unit

        def D_units(st):
            def unit():
                banks = [newbank() for _ in range(4)]

                def mmg(e):
                    ins = None
                    for jj in range(4):
                        for pc in range(4):
                            for hh in range(2):
                                h = 2 * jj + hh
                                ins = e.matmul(pbank[banks[jj]][hh * 64:(hh + 1) * 64, pc * 128:(pc + 1) * 128],
                                               lhsT=vln[:, pc, h * 64:(h + 1) * 64], rhs=WT[:, h, :], start=True, stop=True)
                    return ins
                S.op("pe", mmg, reads=[("vln", pc) for pc in range(4)] + ["WT"], writes=[("pb", b) for b in banks])
                for jj in range(4):
                    b = banks[jj]
                    gb = jj
                    S.op("pool", lambda e, jj=jj, gb=gb: e.tensor_tensor(out=gate[:, gb, :], in0=uT[:, jj, :], in1=zsT[:, jj, :], op=ALU.mult),
                         reads=[("uT", jj), ("zsT", jj)], writes=[("gate", gb)])
                    S.op("dve", lambda e, jj=jj, gb=gb, b=b: e.scalar_tensor_tensor(
                        out=sgt[:, gb, :].rearrange("p (c t) -> p c t", c=4), in0=pbank[b][:].rearrange("p (c t) -> p c t", c=4),
                        scalar=lng[:, jj:jj + 1], in1=Cmat[:, jj, :].unsqueeze(1).to_broadcast([128, 4, 128]), op0=ALU.mult, op1=ALU.add),
                        reads=[("pb", b), "c_lng", ("Cmat", jj)], writes=[("sgt", gb)])
                for jj in range(4):
                    gb = jj
                    S.op("dve", lambda e, jj=jj, gb=gb: e.tensor_tensor(out=mixT[:, 4 + jj, :], in0=sgt[:, gb, :], in1=gate[:, gb, :], op=ALU.mult),
                         reads=[("sgt", gb), ("gate", gb)], writes=[("mixS", jj)])
            return [unit]

        def E_units(st):
            wo_keys = [("w_out", k) for k in range(8)]

            def mk(tt):
                def unit():
                    n = 4 * st + tt
                    slot = n % NX
                    ys = n % 2
                    mkeys = [("mixA", tt)] + [("mixS", jj) for jj in range(4)]
                    banks = [newbank(), newbank()]

                    def mmo(e):
                        ins = None
                        for hf in range(2):
                            for ec in range(8):
                                ins = e.matmul(pbank[banks[hf]][:], lhsT=mixT[:, ec, tt * 128:(tt + 1) * 128],
                                               rhs=w_out[:, ec, hf * 512:(hf + 1) * 512], start=(ec == 0), stop=(ec == 7))
                        return ins
                    S.op("pe", mmo, reads=mkeys + wo_keys, writes=[("pb", banks[0]), ("pb", banks[1])])

                    def resid(e):
                        ins = None
                        for hf in range(2):
                            ins = e.tensor_tensor(out=y_t[:, ys, hf * 512:(hf + 1) * 512], in0=pbank[banks[hf]][:],
                                                  in1=x_t[:, slot, hf * 512:(hf + 1) * 512], op=ALU.add)
                        return ins
                    S.op("dve", resid, reads=[("pb", banks[0]), ("pb", banks[1]), ("x", slot)], writes=[("y", ys)])
                    if n + NX < NT:
                        load_x(n + NX)
                    S.op("act", lambda e: e.activation(out=junk[:], in_=y_t[:, ys, :], func=AF.Square, accum_out=st_fs[:, ys:ys + 1]),
                         reads=[("y", ys)], writes=["junk", ("fs", ys)])
                    S.op("act", lambda e: e.activation(out=st_fr[:, ys:ys + 1], in_=st_fs[:, ys:ys + 1], func=AF.Ln, scale=1.0 / D, bias=eps_col[:, 0:1]),
                         reads=[("fs", ys), "eps_col"], writes=[("fr", ys)])
                    S.op("act", lambda e: e.activation(out=st_fr[:, ys:ys + 1], in_=st_fr[:, ys:ys + 1], func=AF.Exp, scale=-0.5),
                         reads=[("fr", ys)], writes=[("fr", ys)])

                def unit_b():
                    n = 4 * st + tt
                    ys = n % 2
                    S.op("dve", lambda e: e.scalar_tensor_tensor(out=y_t[:, ys, :], in0=y_t[:, ys, :], scalar=st_fr[:, ys:ys + 1], in1=gf[:],
                                                                 op0=ALU.mult, op1=ALU.mult),
                         reads=[("y", ys), ("fr", ys), "c_gf"], writes=[("y", ys)])
                    S.op("sp", lambda e: e.dma_start(out=out_d[n * 128:(n + 1) * 128, :], in_=y_t[:, ys, :]),
                         reads=[("y", ys)], writes=[("y", ys), ("out", n)], dsem=d_y[ys])
                return unit, unit_b
            return [mk(tt) for tt in range(4)]

        def run(units):
            for u in units:
                u()

        run(A_pre_units(0))
        run(A_pe_units(0))
        stage_piece(0)
        stage_piece(1)
        stage_piece(2)
        load_consts_late()
        for n in range(4, 8):
            load_x(n)
        load_w_out()
        run(BQKV_units(0))
        setup_late()
        br0 = Brest_units(0)
        run(br0[:6])
        run(A_pre_units(1, use_pool=True, part=1))
        warm(AF.Silu)
        run(br0[8:10])
        run(D_units(0))
        run(A_pre_units(1, use_pool=True, part=2))
        run(br0[6:8])
        warm(AF.Exp)
        run(XB_units(0))
        for st in range(NS):
            nxt = st + 1 < NS
            F = (A_pe_units(st + 1) + BQKV_units(st + 1)) if nxt else []
            fi = [0]

            def f(k):
                for _ in range(k):
                    if fi[0] < len(F):
                        F[fi[0]]()
                        fi[0] += 1
            Eu = E_units(st)
            f(1)
            C1_unit(st, 0)()
            f(1)
            C1_unit(st, 1)()
            C2_unit(st, 0)()
            f(1)
            C1_unit(st, 2)()
            C2_unit(st, 1)()
            Eu[0][0]()
            f(1)
            C1_unit(st, 3)()
            Eu[0][1]()
            C2_unit(st, 2)()
            Eu[1][0]()
            f(1)
            C2_unit(st, 3)()
            Eu[1][1]()
            Eu[2][0]()
            f(100)
            Eu[2][1]()
            Eu[3][0]()
            Eu[3][1]()
            if nxt:
                warm(AF.Gelu)
                br = Brest_units(st + 1)
                run(XB_units(st + 1))
                run(br[:6])
                if st + 2 < NS:
                    run(A_pre_units(st + 2, use_pool=True, part=1))
                warm(AF.Silu)
                run(br[8:10])
                run(D_units(st + 1))
                if st + 2 < NS:
                    run(A_pre_units(st + 2, use_pool=True, part=2))
                run(br[6:8])
                warm(AF.Exp)

        S.final_wait("sp", [("out", n) for n in range(NT)])
        with nc.Block() as block:
            S.emit(block)
    return nc


def _host_layout(inp):
    w_in = np.asarray(inp["w_in"])[0]
    b_in = np.asarray(inp["b_in"])[0]
    w_out = np.asarray(inp["w_out"])[0]
    hp = np.concatenate([np.r_[c * 64:(c + 1) * 64, (c + 4) * 64:(c + 5) * 64] for c in range(4)])
    cols = np.concatenate([hp, np.arange(512, 640), np.arange(640, 768), np.arange(1280, 1792), np.arange(1792, 2304),
                           768 + hp, np.arange(2304, 2816)])
    w_in_p = np.ascontiguousarray(w_in[:, cols])
    b_in_p = b_in[cols]
    rows = np.concatenate([hp, np.arange(512, 1024)])
    w_out_p = np.ascontiguousarray(w_out[rows, :])
    f32 = np.float32
    shared = {
        "w_in": w_in_p,
        "w_out": w_out_p,
        "gcol": np.ascontiguousarray(np.asarray(inp["norm_g"])[0].reshape(8, 128).T),
        "bias_fm": np.ascontiguousarray(np.stack([b_in_p[c0:c0 + 128] for (c0, _) in FM_COLS], axis=1)),
        "bv_bc": np.ascontiguousarray(np.broadcast_to(b_in_p[OFF_V:OFF_V + 128], (128, 128))),
        "bvs_row": np.ascontiguousarray(b_in_p[OFF_VS:OFF_VS + 512].reshape(1, 512)),
        "sinks": np.ascontiguousarray(np.repeat(np.asarray(inp["attn_sinks"])[0].reshape(2, 1, 4), 64, axis=1).reshape(128, 4)),
        "sgu_wT": np.ascontiguousarray(np.asarray(inp["sgu_w"])[0].transpose(2, 0, 1)),
        "tril": np.triu(np.ones((128, 128), f32)),
        "bsB": np.ascontiguousarray(np.repeat(np.asarray(inp["sgu_b"])[0].reshape(4, 2, 1, 128), 64, axis=2)
                                    .reshape(4, 128, 128).transpose(1, 0, 2)),
        "lnb_col": np.ascontiguousarray(np.asarray(inp["sgu_ln_b"])[0].reshape(4, 128).T),
        "lng_col": np.ascontiguousarray(np.asarray(inp["sgu_ln_g"])[0].reshape(4, 128).T),
        "bout_bc": np.ascontiguousarray(np.broadcast_to(np.asarray(inp["b_out"])[0], (128, D))),
        "gf_bc": np.ascontiguousarray(np.broadcast_to(np.asarray(inp["final_norm_g"]), (128, D))),
        "ident": np.eye(128, dtype=f32).astype(ml_dtypes.bfloat16),
    }
    mC = np.triu(np.ones((128, 128), f32))
    mP = 1.0 - mC
    m = np.stack([np.tile(mP, (1, 4)), np.tile(mC, (1, 4))], axis=1)
    shared["mask"] = np.ascontiguousarray(m).astype(ml_dtypes.bfloat16)
    return {k: (v if v.dtype == ml_dtypes.bfloat16 else np.ascontiguousarray(v, dtype=f32)) for k, v in shared.items()}


_NC_CACHE = {}


def kernel(**inputs):
    x = np.asarray(inputs["x"], dtype=np.float32)
    shared = _host_layout(inputs)
    if "nc" not in _NC_CACHE:
        _NC_CACHE["nc"] = build_program()
    nc = _NC_CACHE["nc"]
    in_maps = []
    for c in range(NCORES):
        m = dict(shared)
        m["x"] = np.ascontiguousarray(x[c])
        in_maps.append(m)
    res = run_bass_kernel_spmd(nc, in_maps, core_ids=list(range(NCORES)))
    out = np.stack([np.asarray(res.results[c]["out"], dtype=np.float32) for c in range(NCORES)], axis=0)
    return out
```

```python
from contextlib import ExitStack

import numpy as np
import ml_dtypes
import concourse.bass as bass
import concourse.mybir as mybir
from concourse.bass_utils import run_bass_kernel_spmd

F32 = mybir.dt.float32
BF16 = mybir.dt.bfloat16
AF = mybir.ActivationFunctionType
ALU = mybir.AluOpType

D = 1024
T = 4096
NCORES = 8
NT = T // 128
NS = T // 512
INW = 2816
EPS = 1e-5
NX = 8
OFF_Q, OFF_K, OFF_V, OFF_U, OFF_VS, OFF_ZA, OFF_ZS = 0, 512, 640, 768, 1280, 1792, 2304
PIECES = [(0, 768), (768, 1792), (1792, 2816)]
FM_COLS = ([(OFF_Q + 128 * c, 0) for c in range(4)] + [(OFF_K, 0)] + [(OFF_U + 128 * c, 1) for c in range(4)]
           + [(OFF_ZA + 128 * c, 2) for c in range(4)] + [(OFF_ZS + 128 * c, 2) for c in range(4)])
NFM = len(FM_COLS)


class Sched:
    ENGS = ("pe", "act", "dve", "pool", "sp")

    def __init__(self, eng_sems):
        self.prog = {e: [] for e in self.ENGS}
        self.eng_sem = eng_sems
        self.cnt = {}
        self.seen = {e: {} for e in self.ENGS}
        self.lw = {}
        self.rd = {}

    def op(self, eng, fn, reads=(), writes=(), dsem=None):
        evs = []
        for k in reads:
            if k in self.lw:
                evs.append(self.lw[k])
        for k in writes:
            if k in self.lw:
                evs.append(self.lw[k])
            evs.extend(self.rd.get(k, ()))
        need = {}
        for (sem, val, e2) in evs:
            if e2 == "pe" and eng == "pe":
                continue
            k = id(sem)
            if self.seen[eng].get(k, 0) >= val:
                continue
            if need.get(k, (None, 0))[1] < val:
                need[k] = (sem, val)
        waits = list(need.values())
        for (sem, val) in waits:
            self.seen[eng][id(sem)] = val
        sem, inc = (self.eng_sem[eng], 1) if dsem is None else (dsem, 16)
        self.cnt[id(sem)] = self.cnt.get(id(sem), 0) + inc
        ev = (sem, self.cnt[id(sem)], eng if dsem is None else "dma")
        self.prog[eng].append((fn, waits, (sem, inc)))
        for k in reads:
            self.rd.setdefault(k, []).append(ev)
        for k in writes:
            self.lw[k] = ev
            self.rd[k] = []
        return ev

    def regroup(self, keys):
        last = max((self.lw[k] for k in keys), key=lambda ev: ev[1])
        for k in keys:
            self.lw[k] = last

    def final_wait(self, eng, keys):
        waits = {}
        for k in keys:
            sem, val, _ = self.lw[k]
            if waits.get(id(sem), (None, 0))[1] < val:
                waits[id(sem)] = (sem, val)
        self.prog[eng].append((None, list(waits.values()), None))

    def emit(self, block):
        def run(engine, items):
            for fn, waits, inc in items:
                for (sem, val) in waits:
                    engine.wait_ge(sem, val)
                if fn is None:
                    continue
                fn(engine).then_inc(inc[0], inc[1])

        block.tensor(lambda e: run(e, self.prog["pe"]))
        block.scalar(lambda e: run(e, self.prog["act"]))
        block.vector(lambda e: run(e, self.prog["dve"]))
        block.gpsimd(lambda e: run(e, self.prog["pool"]))
        block.sync(lambda e: run(e, self.prog["sp"]))


def build_program():
    nc = bass.Bass("TRN2", target_bir_lowering=False)

    def din(name, shape, dt=F32):
        return nc.dram_tensor(name, list(shape), dt, kind="ExternalInput").ap()

    x_d = din("x", [T, D])
    win_d = din("w_in", [D, INW])
    wout_d = din("w_out", [D, D])
    gcol_d = din("gcol", [128, 8])
    biasfm_d = din("bias_fm", [128, NFM])
    bv_d = din("bv_bc", [128, 128])
    bvs_d = din("bvs_row", [1, 512])
    sink_d = din("sinks", [128, 4])
    wT_d = din("sgu_wT", [128, 8, 128])
    tril_d = din("tril", [128, 128])
    bsB_d = din("bsB", [128, 4, 128])
    lnb_d = din("lnb_col", [128, 4])
    lng_d = din("lng_col", [128, 4])
    bout_d = din("bout_bc", [128, D])
    gf_d = din("gf_bc", [128, D])
    ident_d = din("ident", [128, 128], BF16)
    mask_d = din("mask", [128, 2, 512], BF16)
    out_d = nc.dram_tensor("out", [T, D], F32, kind="ExternalOutput").ap()

    with ExitStack() as es:
        def sb(name, shape, dt):
            return es.enter_context(nc.sbuf_tensor(name, list(shape), dt))

        def ps(name, shape, dt):
            return es.enter_context(nc.psum_tensor(name, list(shape), dt))

        def sem(name):
            return es.enter_context(nc.semaphore(name))

        w_in = sb("w_in_bf", [128, 8, INW], BF16)
        w_out = sb("w_out_bf", [128, 8, D], BF16)
        x_t = sb("x_t", [128, NX, D], F32)
        xn = sb("xn_bf", [128, 4, D], BF16)
        xT = sb("xT", [128, 8, 512], BF16)
        qT = sb("qT", [128, 2, 4, 512], BF16)
        kT = sb("kT", [128, 8, 128], BF16)
        v_r = sb("v_r", [128, 8, 128], BF16)
        zaT = sb("zaT", [128, 2, 4, 512], BF16)
        uT = sb("uT", [128, 4, 512], BF16)
        zsT = sb("zsT", [128, 4, 512], BF16)
        vs_f = sb("vs_f", [128, 2, 512], F32)
        vln = sb("vln", [128, 4, 512], BF16)
        PT = sb("PT", [128, 2, 2, 2, 512], BF16)
        LR = sb("LR", [128, 2, 512], F32)
        mixT = sb("mixT", [128, 8, 512], BF16)
        sgt = sb("sgt", [128, 4, 512], F32)
        gate = sb("gate", [128, 4, 512], BF16)
        y_t = sb("y_t", [128, 2, D], F32)
        junk = sb("junk", [128, D], BF16)
        ident = sb("ident_sb", [128, 128], BF16)
        mask = sb("mask_sb", [128, 2, 512], BF16)
        ones64 = sb("ones64", [128, 64], BF16)
        ones_row = sb("ones_row", [1, 128], BF16)
        esink_col = sb("esink_col", [128, 4], F32)
        gcol = sb("gcol_sb", [128, 8], F32)
        bias_fm = sb("bias_fm_sb", [128, NFM], F32)
        bv_bc = sb("bv_bc_sb", [128, 128], F32)
        bvs_f = sb("bvs_f", [1, 512], F32)
        bvs_row = sb("bvs_row_sb", [1, 512], BF16)
        WT = sb("WT_bf", [128, 8, 128], BF16)
        tril = sb("tril_sb", [128, 128], F32)
        bsB = sb("bsB_sb", [128, 4, 128], F32)
        lnb = sb("lnb_sb", [128, 4], F32)
        lng = sb("lng_sb", [128, 4], F32)
        Cmat = sb("Cmat", [128, 4, 128], F32)
        bout = sb("bout_sb", [128, D], F32)
        gf = sb("gf_sb", [128, D], F32)
        nh = sb("neghalf", [128, 1], F32)
        dmy = sb("dmy", [128, 2], F32)
        st_ss = sb("st_ss", [128, 4], F32)
        st_ms = sb("st_ms", [128, 4], F32)
        st_rs = sb("st_rs", [128, 4], F32)
        st_bn = sb("st_bn", [128, 2, 6], F32)
        st_mv = sb("st_mv", [128, 2, 2], F32)
        st_ve = sb("st_ve", [128, 2], F32)
        st_lr = sb("st_lr", [128, 2], F32)
        st_fs = sb("st_fs", [128, 2], F32)
        st_fm = sb("st_fm", [128, 2], F32)
        st_fr = sb("st_fr", [128, 2], F32)
        NB = 8
        pbank = [ps(f"pb{i}", [128, 512], F32) for i in range(NB)]
        esems = {e: sem("s_" + e) for e in ("pe", "act", "dve", "pool")}
        d_const = sem("d_const")
        d_x = [sem(f"d_x{i}") for i in range(NX)]
        d_wo = sem("d_wo")
        d_wt = sem("d_wt")
        d_const2 = sem("d_const2")
        d_y = [sem(f"d_y{i}") for i in range(2)]

        S = Sched(esems)
        bank_ctr = [0]

        def newbank():
            b = bank_ctr[0] % NB
            bank_ctr[0] += 1
            return b

        consts = [
            ("c_gcol", gcol[:], gcol_d), ("c_biasfm", bias_fm[:], biasfm_d), ("c_bv", bv_bc[:], bv_d),
            ("c_bvs", bvs_f[:], bvs_d), ("c_sink", esink_col[:], sink_d), ("c_tril", tril[:], tril_d),
            ("c_bsB", bsB[:], bsB_d), ("c_lnb", lnb[:], lnb_d), ("c_lng", lng[:], lng_d),
            ("c_ident", ident[:], ident_d), ("c_mask", mask[:], mask_d),
        ]
        consts_late = [("c_bout", bout[:], bout_d), ("c_gf", gf[:], gf_d)]

        def load_x(n):
            slot = n % NX
            S.op("sp", lambda e: e.dma_start(out=x_t[:, slot, :], in_=x_d[n * 128:(n + 1) * 128, :]),
                 writes=[("x", slot)], dsem=d_x[slot])

        for n in range(4):
            load_x(n)
        for key, dst, src in consts:
            S.op("sp", lambda e, dst=dst, src=src: e.dma_start(out=dst, in_=src), writes=[key], dsem=d_const)
        S.regroup([c[0] for c in consts])
        wt_stage = sgt[:, 0:2, :].rearrange("p a (h t) -> p (a h) t", t=128)
        S.op("sp", lambda e: e.dma_start(out=wt_stage, in_=wT_d), writes=[("sgt", 0), ("sgt", 1)], dsem=d_wt)

        def load_consts_late():
            for key, dst, src in consts_late:
                S.op("sp", lambda e, dst=dst, src=src: e.dma_start(out=dst, in_=src), writes=[key], dsem=d_const2)
            S.regroup([c[0] for c in consts_late])

        S.op("pool", lambda e: e.memset(nh[:], -0.5), writes=["nh"])
        S.op("pool", lambda e: e.memset(dmy[:], 1.0), writes=["dmy"])

        def warm(func):
            S.op("act", lambda e: e.activation(out=dmy[:, 1:2], in_=dmy[:, 0:1], func=func), reads=["dmy"], writes=["dmy1"])
        S.op("pool", lambda e: e.memset(ones64[:], 1.0), writes=["ones64"])
        S.op("pool", lambda e: e.memset(ones_row[:], 1.0), writes=["ones_row"])

        win_v = win_d.rearrange("(k p) c -> k p c", p=128)
        conv_eng = ["pool", "pool", "pool", "pool"]
        stage_i = [0]

        def stage_piece(pc):
            c0, c1 = PIECES[pc]
            for k in range(8):
                i = stage_i[0]
                stage_i[0] += 1
                si = i % 6
                if si < 4:
                    skey, ssem, src = ("x", 4 + si), d_x[4 + si], x_t[:, 4 + si, 0:c1 - c0]
                else:
                    skey, ssem, src = ("y", si - 4), d_y[si - 4], y_t[:, si - 4, 0:c1 - c0]
                S.op("sp", lambda e, k=k, src=src: e.dma_start(out=src, in_=win_v[k, :, c0:c1]),
                     writes=[skey], dsem=ssem)
                ce = conv_eng[i % 4]
                dst = w_in[:, k, c0:c1]
                if ce == "dve":
                    fn = lambda e, dst=dst, src=src, k=k: e.tensor_scalar(out=dst, in0=src, scalar1=gcol[:, k:k + 1], scalar2=None, op0=ALU.mult)
                elif ce == "act":
                    fn = lambda e, dst=dst, src=src, k=k: e.activation(out=dst, in_=src, func=AF.Copy, scale=gcol[:, k:k + 1])
                else:
                    fn = lambda e, dst=dst, src=src, k=k: e.tensor_scalar(out=dst, in0=src, scalar1=gcol[:, k:k + 1], scalar2=1.0, op0=ALU.mult, op1=ALU.mult)
                S.op(ce, fn, reads=[skey, "c_gcol"], writes=[("w_in", k, pc)])

        def load_w_out():
            wout_v = wout_d.rearrange("(k p) c -> k p c", p=128)
            for k in range(8):
                S.op("pool", lambda e, k=k: e.dma_start(out=w_out[:, k, :], in_=wout_v[k]), writes=[("w_out", k)], dsem=d_wo)
            S.regroup([("w_out", k) for k in range(8)])

        def setup_late():
            S.op("dve", lambda e: e.tensor_tensor(out=WT[:], in0=wt_stage, in1=tril[:].unsqueeze(1).to_broadcast([128, 8, 128]), op=ALU.mult),
                 reads=[("sgt", 0), ("sgt", 1), "c_tril"], writes=["WT"])
            S.op("act", lambda e: e.activation(out=esink_col[:], in_=esink_col[:], func=AF.Exp), reads=["c_sink"], writes=["esink_col"])
            S.op("dve", lambda e: e.tensor_copy(out=bvs_row[:], in_=bvs_f[:]), reads=["c_bvs"], writes=["bvs_row"])
            cb = newbank()

            def cmat_mm(e):
                ins = None
                for jj in range(4):
                    for hh in range(2):
                        ins = e.matmul(pbank[cb][hh * 64:(hh + 1) * 64, jj * 128:(jj + 1) * 128], lhsT=ones64[:, :],
                                       rhs=WT[:, 2 * jj + hh, :], start=True, stop=True)
                return ins
            S.op("pe", cmat_mm, reads=["ones64", "WT"], writes=[("pb", cb)])
            for jj in range(4):
                S.op("dve", lambda e, jj=jj: e.scalar_tensor_tensor(out=Cmat[:, jj, :], in0=pbank[cb][:, jj * 128:(jj + 1) * 128],
                                                                    scalar=lnb[:, jj:jj + 1], in1=bsB[:, jj, :], op0=ALU.mult, op1=ALU.add),
                     reads=[("pb", cb), "c_lnb", "c_bsB"], writes=[("Cmat", jj)])

        def wkeys(pc):
            return [("w_in", k, pc) for k in range(8)]

        def A_pre_units(st, use_pool=False, part=0):
            def mk(tt):
                def unit():
                    n = 4 * st + tt
                    slot = n % NX
                    xs = x_t[:, slot, :]
                    if part != 2:
                        S.op("act", lambda e: e.activation(out=junk[:], in_=xs, func=AF.Square, accum_out=st_ss[:, tt:tt + 1]),
                             reads=[("x", slot)], writes=["junk", ("ss", tt)])
                    if use_pool and part == 2:
                        S.op("act", lambda e: e.activation(out=xn[:, tt, :], in_=xs, func=AF.Copy, scale=st_rs[:, tt:tt + 1]),
                             reads=[("x", slot), ("rs", tt)], writes=[("xn", tt)])
                    elif use_pool:
                        S.op("pool", lambda e: e.tensor_scalar(out=st_ms[:, tt:tt + 1], in0=st_ss[:, tt:tt + 1], scalar1=1.0 / D, scalar2=EPS,
                                                               op0=ALU.mult, op1=ALU.add),
                             reads=[("ss", tt)], writes=[("ms", tt)])
                        S.op("pool", lambda e: e.tensor_tensor(out=st_rs[:, tt:tt + 1], in0=st_ms[:, tt:tt + 1], in1=nh[:], op=ALU.pow),
                             reads=[("ms", tt), "nh"], writes=[("rs", tt)])
                        if part == 0:
                            S.op("act", lambda e: e.activation(out=xn[:, tt, :], in_=xs, func=AF.Copy, scale=st_rs[:, tt:tt + 1]),
                                 reads=[("x", slot), ("rs", tt)], writes=[("xn", tt)])
                    else:
                        S.op("dve", lambda e: e.tensor_scalar(out=st_ms[:, tt:tt + 1], in0=st_ss[:, tt:tt + 1], scalar1=1.0 / D, scalar2=EPS,
                                                              op0=ALU.mult, op1=ALU.add),
                             reads=[("ss", tt)], writes=[("ms", tt)])
                        S.op("act", lambda e: e.activation(out=st_rs[:, tt:tt + 1], in_=st_ms[:, tt:tt + 1], func=AF.Ln),
                             reads=[("ms", tt)], writes=[("rs", tt)])
                        S.op("act", lambda e: e.activation(out=st_rs[:, tt:tt + 1], in_=st_rs[:, tt:tt + 1], func=AF.Exp, scale=-0.5),
                             reads=[("rs", tt)], writes=[("rs", tt)])
                        S.op("dve", lambda e: e.tensor_scalar(out=xn[:, tt, :], in0=xs, scalar1=st_rs[:, tt:tt + 1], scalar2=None, op0=ALU.mult),
                             reads=[("x", slot), ("rs", tt)], writes=[("xn", tt)])
                return unit
            return [mk(tt) for tt in range(4)]

        def XB_units(st):
            def mk(tt):
                def unit():
                    slot = (4 * st + tt) % NX
                    xs = x_t[:, slot, :]
                    S.op("pool", lambda e: e.tensor_tensor(out=xs, in0=xs, in1=bout[:], op=ALU.add),
                         reads=[("x", slot), "c_bout"], writes=[("x", slot)])
                return unit
            return [mk(tt) for tt in range(4)]

        def A_pe_units(st):
            def mk(pair):
                def unit():
                    tts = (2 * pair, 2 * pair + 1)
                    banks = [newbank(), newbank()]
                    pts = [pbank[b][:].bitcast(BF16).rearrange("p (c t) -> p c t", c=8) for b in banks]

                    def tr(e):
                        ins = None
                        for i, tt in enumerate(tts):
                            for c in range(8):
                                ins = e.transpose(pts[i][:, c, :], xn[:, tt, c * 128:(c + 1) * 128], ident[:])
                        return ins
                    S.op("pe", tr, reads=[("xn", tt) for tt in tts] + ["c_ident"], writes=[("pb", b) for b in banks])
                    for i, tt in enumerate(tts):
                        S.op("dve", lambda e, i=i, tt=tt: e.tensor_copy(out=xT[:, :, tt * 128:(tt + 1) * 128], in_=pts[i]),
                             reads=[("pb", banks[i])], writes=[("xT", tt)])
                return unit
            return [mk(0), mk(1)]

        xT_keys = [("xT", tt) for tt in range(4)]

        def fm_unit(fis, func, dsts, keys):
            def unit():
                banks = [newbank() for _ in fis]

                def mm(e):
                    ins = None
                    for fi, b in zip(fis, banks):
                        col = FM_COLS[fi][0]
                        for k in range(8):
                            ins = e.matmul(pbank[b][:], lhsT=w_in[:, k, col:col + 128], rhs=xT[:, k, :], start=(k == 0), stop=(k == 7))
                    return ins
                pcs = sorted({FM_COLS[fi][1] for fi in fis})
                S.op("pe", mm, reads=xT_keys + [kk for pc in pcs for kk in wkeys(pc)], writes=[("pb", b) for b in banks])
                for fi, b, dst, key in zip(fis, banks, dsts, keys):
                    if func == AF.Identity:
                        S.op("dve", lambda e, fi=fi, b=b, dst=dst: e.tensor_scalar(out=dst, in0=pbank[b][:], scalar1=bias_fm[:, fi:fi + 1],
                                                                                   scalar2=None, op0=ALU.add),
                             reads=[("pb", b), "c_biasfm"], writes=[key])
                    else:
                        S.op("act", lambda e, fi=fi, b=b, dst=dst: e.activation(out=dst, in_=pbank[b][:], func=func, bias=bias_fm[:, fi:fi + 1]),
                             reads=[("pb", b), "c_biasfm"], writes=[key])
            return unit

        def BQKV_units(st):
            qb = st % 2
            us = [fm_unit([c, c + 1], AF.Identity, [qT[:, qb, c, :], qT[:, qb, c + 1, :]], [("qT", qb, c), ("qT", qb, c + 1)]) for c in (0, 2)]
            kslot = (st % 2) * 4
            us.append(fm_unit([4], AF.Identity, [kT[:, kslot:kslot + 4, :].rearrange("p a b -> p (a b)")], [("kT", st % 2)]))

            def vunit():
                b = newbank()
                vslot = (st % 2) * 4

                def mmv(e):
                    ins = None
                    for tt in range(4):
                        for k in range(8):
                            ins = e.matmul(pbank[b][:, tt * 128:(tt + 1) * 128], lhsT=xT[:, k, tt * 128:(tt + 1) * 128],
                                           rhs=w_in[:, k, OFF_V:OFF_V + 128], start=(k == 0), stop=(k == 7))
                    return ins
                S.op("pe", mmv, reads=xT_keys + wkeys(0), writes=[("pb", b)])
                S.op("dve", lambda e: e.tensor_tensor(out=v_r[:, vslot:vslot + 4, :], in0=pbank[b][:].rearrange("p (a c) -> p a c", a=4),
                                                      in1=bv_bc[:].unsqueeze(1).to_broadcast([128, 4, 128]), op=ALU.add),
                     reads=[("pb", b), "c_bv"], writes=[("v", vslot + i) for i in range(4)])
            us.append(vunit)
            return us

        def Brest_units(st):
            zb = st % 2
            us = [fm_unit([5 + c, 6 + c], AF.Gelu, [uT[:, c, :], uT[:, c + 1, :]], [("uT", c), ("uT", c + 1)]) for c in (0, 2)]

            def mkvs(tt):
                def unit():
                    b = newbank()
                    vb = tt % 2

                    def mmvs(e):
                        for k in range(8):
                            e.matmul(pbank[b][:], lhsT=xT[:, k, tt * 128:(tt + 1) * 128], rhs=w_in[:, k, OFF_VS:OFF_VS + 512],
                                     start=(k == 0), stop=False)
                        return e.matmul(pbank[b][:], lhsT=ones_row[0:1, :], rhs=bvs_row[0:1, :], start=False, stop=True)
                    S.op("pe", mmvs, reads=[("xT", tt), "ones_row", "bvs_row"] + wkeys(1), writes=[("pb", b)])
                    S.op("act", lambda e: e.activation(out=vs_f[:, vb, :], in_=pbank[b][:], func=AF.Gelu),
                         reads=[("pb", b)], writes=[("vs_f", vb)])
                    S.op("dve", lambda e: e.bn_stats(out=st_bn[:, vb, :], in_=vs_f[:, vb, :]), reads=[("vs_f", vb)], writes=[("bn", vb)])
                    S.op("dve", lambda e: e.bn_aggr(out=st_mv[:, vb, :], in_=st_bn[:, vb, :]), reads=[("bn", vb)], writes=[("mv", vb)])
                    S.op("dve", lambda e: e.tensor_scalar(out=st_ve[:, vb:vb + 1], in0=st_mv[:, vb, 1:2], scalar1=EPS, scalar2=None, op0=ALU.add),
                         reads=[("mv", vb)], writes=[("ve", vb)])
                    S.op("pool", lambda e: e.tensor_tensor(out=st_lr[:, vb:vb + 1], in0=st_ve[:, vb:vb + 1], in1=nh[:], op=ALU.pow),
                         reads=[("ve", vb), "nh"], writes=[("lr", vb)])
                    S.op("dve", lambda e: e.tensor_scalar(out=vln[:, tt, :], in0=vs_f[:, vb, :], scalar1=st_mv[:, vb, 0:1],
                                                          scalar2=st_lr[:, vb:vb + 1], op0=ALU.subtract, op1=ALU.mult),
                         reads=[("vs_f", vb), ("mv", vb), ("lr", vb)], writes=[("vln", tt)])
                return unit
            us += [mkvs(tt) for tt in range(4)]
            us += [fm_unit([9 + c, 10 + c], AF.Silu, [zaT[:, zb, c, :], zaT[:, zb, c + 1, :]], [("zaT", zb, c), ("zaT", zb, c + 1)])
                   for c in (0, 2)]
            us += [fm_unit([13 + c, 14 + c], AF.Silu, [zsT[:, c, :], zsT[:, c + 1, :]], [("zsT", c), ("zsT", c + 1)]) for c in (0, 2)]
            return us

        cstate = {}

        def C1_unit(st, j):
            def unit():
                n = 4 * st + j
                qb = st % 2
                kbs = [1] if n == 0 else [0, 1]
                qkeys = [("qT", qb, c) for c in range(4)]
                sbk = {}
                order = [(g, kb) for kb in kbs for g in range(2)]
                for gk in order:
                    sbk[gk] = newbank()

                def mms(e):
                    ins = None
                    for (g, kb) in order:
                        kslot = (n - 1 + kb) % 8
                        ins = e.matmul(pbank[sbk[(g, kb)]][:], lhsT=kT[g * 64:(g + 1) * 64, kslot, :],
                                       rhs=qT[g * 64:(g + 1) * 64, qb, :, j * 128:(j + 1) * 128], start=True, stop=True)
                    return ins
                S.op("pe", mms, reads=qkeys + [("kT", ((n - 1 + kb) // 4) % 2) for kb in kbs], writes=[("pb", b) for b in sbk.values()])
                pp = n % 2
                for kb in kbs:
                    for g in range(2):
                        b = sbk[(g, kb)]
                        S.op("act", lambda e, g=g, kb=kb, b=b: e.activation(out=PT[:, pp, kb, g, :], in_=pbank[b][:], func=AF.Exp, scale=0.125),
                             reads=[("pb", b)], writes=[("PT", pp, kb, g)])
                        S.op("dve", lambda e, g=g, kb=kb: e.tensor_tensor(out=PT[:, pp, kb, g, :], in0=PT[:, pp, kb, g, :], in1=mask[:, kb, :], op=ALU.mult),
                             reads=[("PT", pp, kb, g), "c_mask"], writes=[("PT", pp, kb, g)])
            return unit

        def C2_unit(st, j):
            def unit():
                n = 4 * st + j
                zb = st % 2
                pp = n % 2
                kbs = [1] if n == 0 else [0, 1]
                bn_ = newbank()
                bd_ = newbank()

                def mmpv(e):
                    for g in range(2):
                        for i, kb in enumerate(kbs):
                            kn = n - 1 + kb
                            e.matmul(pbank[bn_][g * 64:(g + 1) * 64, :], lhsT=v_r[:, kn % 8, g * 64:(g + 1) * 64], rhs=PT[:, pp, kb, g, :],
                                     start=(i == 0), stop=(i == len(kbs) - 1))
                    ins = None
                    for g in range(2):
                        for i, kb in enumerate(kbs):
                            ins = e.matmul(pbank[bd_][g * 64:(g + 1) * 64, :], lhsT=ones64[:, :], rhs=PT[:, pp, kb, g, :],
                                           start=(i == 0), stop=(i == len(kbs) - 1))
                    return ins
                vkeys = [("v", (n - 1 + kb) % 8) for kb in kbs]
                S.op("pe", mmpv, reads=[("PT", pp, kb, g) for kb in kbs for g in range(2)] + vkeys + ["ones64"],
                     writes=[("pb", bn_), ("pb", bd_)])
                lb = j % 2
                def lnop(e):
                    ins = None
                    for c in range(4):
                        ins = e.activation(out=LR[:, lb, c * 128:(c + 1) * 128], in_=pbank[bd_][:, c * 128:(c + 1) * 128], func=AF.Ln,
                                           bias=esink_col[:, c:c + 1])
                    return ins
                S.op("act", lnop, reads=[("pb", bd_), "esink_col"], writes=[("LR", lb)])
                S.op("act", lambda e: e.activation(out=LR[:, lb, :], in_=LR[:, lb, :], func=AF.Exp, scale=-1.0), reads=[("LR", lb)], writes=[("LR", lb)])
                S.op("pool", lambda e: e.tensor_tensor(out=LR[:, lb, :].rearrange("p (c q) -> p c q", c=4),
                                                       in0=LR[:, lb, :].rearrange("p (c q) -> p c q", c=4),
                                                       in1=zaT[:, zb, :, j * 128:(j + 1) * 128], op=ALU.mult),
                     reads=[("LR", lb)] + [("zaT", zb, c) for c in range(4)], writes=[("LR", lb)])
                S.op("dve", lambda e: e.tensor_tensor(out=mixT[:, 0:4, j * 128:(j + 1) * 128],
                                                      in0=pbank[bn_][:].rearrange("p (c q) -> p c q", c=4),
                                                      in1=LR[:, lb, :].rearrange("p (c q) -> p c q", c=4), op=ALU.mult),
                     reads=[("pb", bn_), ("LR", lb)], writes=[("mixA", j)])
            return unit

        def D_units(st):
            def unit():
                banks = [newbank() for _ in range(4)]

                def mmg(e):
                    ins = None
                    for jj in range(4):
                        for pc in range(4):
                            for hh in range(2):
                                h = 2 * jj + hh
                                ins = e.matmul(pbank[banks[jj]][hh * 64:(hh + 1) * 64, pc * 128:(pc + 1) * 128],
                                               lhsT=vln[:, pc, h * 64:(h + 1) * 64], rhs=WT[:, h, :], start=True, stop=True)
                    return ins
                S.op("pe", mmg, reads=[("vln", pc) for pc in range(4)] + ["WT"], writes=[("pb", b) for b in banks])
                for jj in range(4):
                    b = banks[jj]
                    gb = jj
                    S.op("pool", lambda e, jj=jj, gb=gb: e.tensor_tensor(out=gate[:, gb, :], in0=uT[:, jj, :], in1=zsT[:, jj, :], op=ALU.mult),
                         reads=[("uT", jj), ("zsT", jj)], writes=[("gate", gb)])
                    S.op("dve", lambda e, jj=jj, gb=gb, b=b: e.scalar_tensor_tensor(
                        out=sgt[:, gb, :].rearrange("p (c t) -> p c t", c=4), in0=pbank[b][:].rearrange("p (c t) -> p c t", c=4),
                        scalar=lng[:, jj:jj + 1], in1=Cmat[:, jj, :].unsqueeze(1).to_broadcast([128, 4, 128]), op0=ALU.mult, op1=ALU.add),
                        reads=[("pb", b), "c_lng", ("Cmat", jj)], writes=[("sgt", gb)])
                for jj in range(4):
                    gb = jj
                    S.op("dve", lambda e, jj=jj, gb=gb: e.tensor_tensor(out=mixT[:, 4 + jj, :], in0=sgt[:, gb, :], in1=gate[:, gb, :], op=ALU.mult),
                         reads=[("sgt", gb), ("gate", gb)], writes=[("mixS", jj)])
            return [unit]

        def E_units(st):
            wo_keys = [("w_out", k) for k in range(8)]

            def mk(tt):
                def unit():
                    n = 4 * st + tt
                    slot = n % NX
                    ys = n % 2
                    mkeys = [("mixA", tt)] + [("mixS", jj) for jj in range(4)]
                    banks = [newbank(), newbank()]

                    def mmo(e):
                        ins = None
                        for hf in range(2):
                            for ec in range(8):
                                ins = e.matmul(pbank[banks[hf]][:], lhsT=mixT[:, ec, tt * 128:(tt + 1) * 128],
                                               rhs=w_out[:, ec, hf * 512:(hf + 1) * 512], start=(ec == 0), stop=(ec == 7))
                        return ins
                    S.op("pe", mmo, reads=mkeys + wo_keys, writes=[("pb", banks[0]), ("pb", banks[1])])

                    def resid(e):
                        ins = None
                        for hf in range(2):
                            ins = e.tensor_tensor(out=y_t[:, ys, hf * 512:(hf + 1) * 512], in0=pbank[banks[hf]][:],
                                                  in1=x_t[:, slot, hf * 512:(hf + 1) * 512], op=ALU.add)
                        return ins
                    S.op("dve", resid, reads=[("pb", banks[0]), ("pb", banks[1]), ("x", slot)], writes=[("y", ys)])
                    if n + NX < NT:
                        load_x(n + NX)
                    S.op("act", lambda e: e.activation(out=junk[:], in_=y_t[:, ys, :], func=AF.Square, accum_out=st_fs[:, ys:ys + 1]),
                         reads=[("y", ys)], writes=["junk", ("fs", ys)])
                    S.op("dve", lambda e: e.tensor_scalar(out=st_fm[:, ys:ys + 1], in0=st_fs[:, ys:ys + 1], scalar1=1.0 / D, scalar2=EPS,
                                                          op0=ALU.mult, op1=ALU.add),
                         reads=[("fs", ys)], writes=[("fm", ys)])
                    S.op("act", lambda e: e.activation(out=st_fr[:, ys:ys + 1], in_=st_fm[:, ys:ys + 1], func=AF.Ln),
                         reads=[("fm", ys)], writes=[("fr", ys)])
                    S.op("act", lambda e: e.activation(out=st_fr[:, ys:ys + 1], in_=st_fr[:, ys:ys + 1], func=AF.Exp, scale=-0.5),
                         reads=[("fr", ys)], writes=[("fr", ys)])
                    S.op("dve", lambda e: e.scalar_tensor_tensor(out=y_t[:, ys, :], in0=y_t[:, ys, :], scalar=st_fr[:, ys:ys + 1], in1=gf[:],
                                                                 op0=ALU.mult, op1=ALU.mult),
                         reads=[("y", ys), ("fr", ys), "c_gf"], writes=[("y", ys)])
                    S.op("sp", lambda e: e.dma_start(out=out_d[n * 128:(n + 1) * 128, :], in_=y_t[:, ys, :]),
                         reads=[("y", ys)], writes=[("y", ys), ("out", n)], dsem=d_y[ys])
                return unit
            return [mk(tt) for tt in range(4)]

        def run(units):
            for u in units:
                u()

        run(A_pre_units(0))
        run(A_pe_units(0))
        stage_piece(0)
        stage_piece(1)
        stage_piece(2)
        load_consts_late()
        for n in range(4, 8):
            load_x(n)
        load_w_out()
        run(BQKV_units(0))
        setup_late()
        br0 = Brest_units(0)
        run(br0[:6])
        run(A_pre_units(1, use_pool=True, part=1))
        warm(AF.Silu)
        run(br0[8:10])
        run(D_units(0))
        run(A_pre_units(1, use_pool=True, part=2))
        C1_unit(0, 0)()
        warm(AF.Silu)
        run(br0[6:8])
        warm(AF.Exp)
        run(XB_units(0))
        for st in range(NS):
            nxt = st + 1 < NS
            F = (A_pe_units(st + 1) + BQKV_units(st + 1)) if nxt else []
            fi = [0]

            def f(k):
                for _ in range(k):
                    if fi[0] < len(F):
                        F[fi[0]]()
                        fi[0] += 1
            Eu = E_units(st)
            f(1)
            f(1)
            C1_unit(st, 1)()
            C2_unit(st, 0)()
            f(1)
            C1_unit(st, 2)()
            C2_unit(st, 1)()
            Eu[0]()
            f(1)
            C1_unit(st, 3)()
            C2_unit(st, 2)()
            Eu[1]()
            f(1)
            C2_unit(st, 3)()
            Eu[2]()
            f(100)
            Eu[3]()
            if nxt:
                warm(AF.Gelu)
                br = Brest_units(st + 1)
                run(XB_units(st + 1))
                run(br[:6])
                if st + 2 < NS:
                    run(A_pre_units(st + 2, use_pool=True, part=1))
                warm(AF.Silu)
                run(br[8:10])
                run(D_units(st + 1))
                if st + 2 < NS:
                    run(A_pre_units(st + 2, use_pool=True, part=2))
                C1_unit(st + 1, 0)()
                warm(AF.Silu)
                run(br[6:8])
                warm(AF.Exp)

        S.final_wait("sp", [("out", n) for n in range(NT)])
        with nc.Block() as block:
            S.emit(block)
    return nc


def _host_layout(inp):
    w_in = np.asarray(inp["w_in"])[0]
    b_in = np.asarray(inp["b_in"])[0]
    w_out = np.asarray(inp["w_out"])[0]
    hp = np.concatenate([np.r_[c * 64:(c + 1) * 64, (c + 4) * 64:(c + 5) * 64] for c in range(4)])
    cols = np.concatenate([hp, np.arange(512, 640), np.arange(640, 768), np.arange(1280, 1792), np.arange(1792, 2304),
                           768 + hp, np.arange(2304, 2816)])
    w_in_p = np.ascontiguousarray(w_in[:, cols])
    b_in_p = b_in[cols]
    rows = np.concatenate([hp, np.arange(512, 1024)])
    w_out_p = np.ascontiguousarray(w_out[rows, :])
    f32 = np.float32
    shared = {
        "w_in": w_in_p,
        "w_out": w_out_p,
        "gcol": np.ascontiguousarray(np.asarray(inp["norm_g"])[0].reshape(8, 128).T),
        "bias_fm": np.ascontiguousarray(np.stack([b_in_p[c0:c0 + 128] for (c0, _) in FM_COLS], axis=1)),
        "bv_bc": np.ascontiguousarray(np.broadcast_to(b_in_p[OFF_V:OFF_V + 128], (128, 128))),
        "bvs_row": np.ascontiguousarray(b_in_p[OFF_VS:OFF_VS + 512].reshape(1, 512)),
        "sinks": np.ascontiguousarray(np.repeat(np.asarray(inp["attn_sinks"])[0].reshape(2, 1, 4), 64, axis=1).reshape(128, 4)),
        "sgu_wT": np.ascontiguousarray(np.asarray(inp["sgu_w"])[0].transpose(2, 0, 1)),
        "tril": np.triu(np.ones((128, 128), f32)),
        "bsB": np.ascontiguousarray(np.repeat(np.asarray(inp["sgu_b"])[0].reshape(4, 2, 1, 128), 64, axis=2)
                                    .reshape(4, 128, 128).transpose(1, 0, 2)),
        "lnb_col": np.ascontiguousarray(np.asarray(inp["sgu_ln_b"])[0].reshape(4, 128).T),
        "lng_col": np.ascontiguousarray(np.asarray(inp["sgu_ln_g"])[0].reshape(4, 128).T),
        "bout_bc": np.ascontiguousarray(np.broadcast_to(np.asarray(inp["b_out"])[0], (128, D))),
        "gf_bc": np.ascontiguousarray(np.broadcast_to(np.asarray(inp["final_norm_g"]), (128, D))),
        "ident": np.eye(128, dtype=f32).astype(ml_dtypes.bfloat16),
    }
    mC = np.triu(np.ones((128, 128), f32))
    mP = 1.0 - mC
    m = np.stack([np.tile(mP, (1, 4)), np.tile(mC, (1, 4))], axis=1)
    shared["mask"] = np.ascontiguousarray(m).astype(ml_dtypes.bfloat16)
    return {k: (v if v.dtype == ml_dtypes.bfloat16 else np.ascontiguousarray(v, dtype=f32)) for k, v in shared.items()}


_NC_CACHE = {}


def kernel(**inputs):
    x = np.asarray(inputs["x"], dtype=np.float32)
    shared = _host_layout(inputs)
    if "nc" not in _NC_CACHE:
        _NC_CACHE["nc"] = build_program()
    nc = _NC_CACHE["nc"]
    in_maps = []
    for c in range(NCORES):
        m = dict(shared)
        m["x"] = np.ascontiguousarray(x[c])
        in_maps.append(m)
    res = run_bass_kernel_spmd(nc, in_maps, core_ids=list(range(NCORES)))
    out = np.stack([np.asarray(res.results[c]["out"], dtype=np.float32) for c in range(NCORES)], axis=0)
    return out
```

```python
from contextlib import ExitStack

import numpy as np
import ml_dtypes
import concourse.bass as bass
import concourse.mybir as mybir
from concourse.bass_utils import run_bass_kernel_spmd

F32 = mybir.dt.float32
BF16 = mybir.dt.bfloat16
AF = mybir.ActivationFunctionType
ALU = mybir.AluOpType

D = 1024
T = 4096
NCORES = 8
NT = T // 128
NS = T // 512
INW = 2816
EPS = 1e-5
NX = 8
OFF_Q, OFF_K, OFF_V, OFF_U, OFF_VS, OFF_ZA, OFF_ZS = 0, 512, 640, 768, 1280, 1792, 2304
PIECES = [(0, 768), (768, 1792), (1792, 2816)]
FM_COLS = ([(OFF_Q + 128 * c, 0) for c in range(4)] + [(OFF_K, 0)] + [(OFF_U + 128 * c, 1) for c in range(4)]
           + [(OFF_ZA + 128 * c, 2) for c in range(4)] + [(OFF_ZS + 128 * c, 2) for c in range(4)])
NFM = len(FM_COLS)


class Sched:
    ENGS = ("pe", "act", "dve", "pool", "sp")

    def __init__(self, eng_sems):
        self.prog = {e: [] for e in self.ENGS}
        self.eng_sem = eng_sems
        self.cnt = {}
        self.seen = {e: {} for e in self.ENGS}
        self.lw = {}
        self.rd = {}

    def op(self, eng, fn, reads=(), writes=(), dsem=None):
        evs = []
        for k in reads:
            if k in self.lw:
                evs.append(self.lw[k])
        for k in writes:
            if k in self.lw:
                evs.append(self.lw[k])
            evs.extend(self.rd.get(k, ()))
        need = {}
        for (sem, val, e2) in evs:
            if e2 == "pe" and eng == "pe":
                continue
            k = id(sem)
            if self.seen[eng].get(k, 0) >= val:
                continue
            if need.get(k, (None, 0))[1] < val:
                need[k] = (sem, val)
        waits = list(need.values())
        for (sem, val) in waits:
            self.seen[eng][id(sem)] = val
        sem, inc = (self.eng_sem[eng], 1) if dsem is None else (dsem, 16)
        self.cnt[id(sem)] = self.cnt.get(id(sem), 0) + inc
        ev = (sem, self.cnt[id(sem)], eng if dsem is None else "dma")
        self.prog[eng].append((fn, waits, (sem, inc)))
        for k in reads:
            self.rd.setdefault(k, []).append(ev)
        for k in writes:
            self.lw[k] = ev
            self.rd[k] = []
        return ev

    def regroup(self, keys):
        last = max((self.lw[k] for k in keys), key=lambda ev: ev[1])
        for k in keys:
            self.lw[k] = last

    def final_wait(self, eng, keys):
        waits = {}
        for k in keys:
            sem, val, _ = self.lw[k]
            if waits.get(id(sem), (None, 0))[1] < val:
                waits[id(sem)] = (sem, val)
        self.prog[eng].append((None, list(waits.values()), None))

    def emit(self, block):
        def run(engine, items):
            for fn, waits, inc in items:
                for (sem, val) in waits:
                    engine.wait_ge(sem, val)
                if fn is None:
                    continue
                fn(engine).then_inc(inc[0], inc[1])

        block.tensor(lambda e: run(e, self.prog["pe"]))
        block.scalar(lambda e: run(e, self.prog["act"]))
        block.vector(lambda e: run(e, self.prog["dve"]))
        block.gpsimd(lambda e: run(e, self.prog["pool"]))
        block.sync(lambda e: run(e, self.prog["sp"]))


def build_program():
    nc = bass.Bass("TRN2", target_bir_lowering=False)

    def din(name, shape, dt=F32):
        return nc.dram_tensor(name, list(shape), dt, kind="ExternalInput").ap()

    x_d = din("x", [T, D])
    win_d = din("w_in", [D, INW])
    wout_d = din("w_out", [D, D])
    gcol_d = din("gcol", [128, 8])
    biasfm_d = din("bias_fm", [128, NFM])
    bv_d = din("bv_bc", [128, 128])
    bvs_d = din("bvs_row", [1, 512])
    sink_d = din("sinks", [128, 4])
    wT_d = din("sgu_wT", [128, 8, 128])
    tril_d = din("tril", [128, 128])
    bsB_d = din("bsB", [128, 4, 128])
    lnb_d = din("lnb_col", [128, 4])
    lng_d = din("lng_col", [128, 4])
    bout_d = din("bout_bc", [128, D])
    gf_d = din("gf_bc", [128, D])
    ident_d = din("ident", [128, 128], BF16)
    mask_d = din("mask", [128, 2, 512], BF16)
    out_d = nc.dram_tensor("out", [T, D], F32, kind="ExternalOutput").ap()

    with ExitStack() as es:
        def sb(name, shape, dt):
            return es.enter_context(nc.sbuf_tensor(name, list(shape), dt))

        def ps(name, shape, dt):
            return es.enter_context(nc.psum_tensor(name, list(shape), dt))

        def sem(name):
            return es.enter_context(nc.semaphore(name))

        w_in = sb("w_in_bf", [128, 8, INW], BF16)
        w_out = sb("w_out_bf", [128, 8, D], BF16)
        x_t = sb("x_t", [128, NX, D], F32)
        xn = sb("xn_bf", [128, 4, D], BF16)
        xT = sb("xT", [128, 8, 512], BF16)
        qT = sb("qT", [128, 2, 4, 512], BF16)
        kT = sb("kT", [128, 8, 128], BF16)
        v_r = sb("v_r", [128, 8, 128], BF16)
        zaT = sb("zaT", [128, 2, 4, 512], BF16)
        uT = sb("uT", [128, 4, 512], BF16)
        zsT = sb("zsT", [128, 4, 512], BF16)
        vs_f = sb("vs_f", [128, 2, 512], F32)
        vln = sb("vln", [128, 4, 512], BF16)
        PT = sb("PT", [128, 2, 2, 2, 512], BF16)
        LR = sb("LR", [128, 2, 512], F32)
        mixT = sb("mixT", [128, 8, 512], BF16)
        sgt = sb("sgt", [128, 4, 512], F32)
        gate = sb("gate", [128, 4, 512], BF16)
        y_t = sb("y_t", [128, 2, D], F32)
        junk = sb("junk", [128, D], BF16)
        ident = sb("ident_sb", [128, 128], BF16)
        mask = sb("mask_sb", [128, 2, 512], BF16)
        ones64 = sb("ones64", [128, 64], BF16)
        ones_row = sb("ones_row", [1, 128], BF16)
        esink_col = sb("esink_col", [128, 4], F32)
        gcol = sb("gcol_sb", [128, 8], F32)
        bias_fm = sb("bias_fm_sb", [128, NFM], F32)
        bv_bc = sb("bv_bc_sb", [128, 128], F32)
        bvs_f = sb("bvs_f", [1, 512], F32)
        bvs_row = sb("bvs_row_sb", [1, 512], BF16)
        WT = sb("WT_bf", [128, 8, 128], BF16)
        tril = sb("tril_sb", [128, 128], F32)
        bsB = sb("bsB_sb", [128, 4, 128], F32)
        lnb = sb("lnb_sb", [128, 4], F32)
        lng = sb("lng_sb", [128, 4], F32)
        Cmat = sb("Cmat", [128, 4, 128], F32)
        bout = sb("bout_sb", [128, D], F32)
        gf = sb("gf_sb", [128, D], F32)
        nh = sb("neghalf", [128, 1], F32)
        dmy = sb("dmy", [128, 2], F32)
        st_ss = sb("st_ss", [128, 4], F32)
        st_ms = sb("st_ms", [128, 4], F32)
        st_rs = sb("st_rs", [128, 4], F32)
        st_bn = sb("st_bn", [128, 2, 6], F32)
        st_mv = sb("st_mv", [128, 2, 2], F32)
        st_ve = sb("st_ve", [128, 2], F32)
        st_lr = sb("st_lr", [128, 2], F32)
        st_fs = sb("st_fs", [128, 2], F32)
        st_fm = sb("st_fm", [128, 2], F32)
        st_fr = sb("st_fr", [128, 2], F32)
        NB = 8
        pbank = [ps(f"pb{i}", [128, 512], F32) for i in range(NB)]
        esems = {e: sem("s_" + e) for e in ("pe", "act", "dve", "pool")}
        d_const = sem("d_const")
        d_x = [sem(f"d_x{i}") for i in range(NX)]
        d_wo = sem("d_wo")
        d_wt = sem("d_wt")
        d_c0 = sem("d_c0")
        d_const2 = sem("d_const2")
        d_y = [sem(f"d_y{i}") for i in range(2)]

        S = Sched(esems)
        bank_ctr = [0]

        def newbank():
            b = bank_ctr[0] % NB
            bank_ctr[0] += 1
            return b

        consts0 = [("c_gcol", gcol[:], gcol_d), ("c_ident", ident[:], ident_d), ("c_biasfm", bias_fm[:], biasfm_d),
                   ("c_bv", bv_bc[:], bv_d)]
        consts = [
            ("c_bvs", bvs_f[:], bvs_d), ("c_sink", esink_col[:], sink_d), ("c_tril", tril[:], tril_d),
            ("c_bsB", bsB[:], bsB_d), ("c_lnb", lnb[:], lnb_d), ("c_lng", lng[:], lng_d),
            ("c_mask", mask[:], mask_d),
        ]
        consts_late = [("c_bout", bout[:], bout_d), ("c_gf", gf[:], gf_d)]

        def load_x(n):
            slot = n % NX
            S.op("sp", lambda e: e.dma_start(out=x_t[:, slot, :], in_=x_d[n * 128:(n + 1) * 128, :]),
                 writes=[("x", slot)], dsem=d_x[slot])

        for n in range(4):
            load_x(n)
        for key, dst, src in consts0:
            S.op("sp", lambda e, dst=dst, src=src: e.dma_start(out=dst, in_=src), writes=[key], dsem=d_c0)
        S.regroup([c[0] for c in consts0])
        wt_stage = sgt[:, 0:2, :].rearrange("p a (h t) -> p (a h) t", t=128)

        def load_consts_mid():
            for key, dst, src in consts:
                S.op("sp", lambda e, dst=dst, src=src: e.dma_start(out=dst, in_=src), writes=[key], dsem=d_const)
            S.regroup([c[0] for c in consts])
            S.op("sp", lambda e: e.dma_start(out=wt_stage, in_=wT_d), writes=[("sgt", 0), ("sgt", 1)], dsem=d_wt)

        def load_consts_late():
            for key, dst, src in consts_late:
                S.op("sp", lambda e, dst=dst, src=src: e.dma_start(out=dst, in_=src), writes=[key], dsem=d_const2)
            S.regroup([c[0] for c in consts_late])

        S.op("pool", lambda e: e.memset(nh[:], -0.5), writes=["nh"])
        S.op("pool", lambda e: e.memset(dmy[:], 1.0), writes=["dmy"])

        def warm(func):
            S.op("act", lambda e: e.activation(out=dmy[:, 1:2], in_=dmy[:, 0:1], func=func), reads=["dmy"], writes=["dmy1"])
        S.op("pool", lambda e: e.memset(ones64[:], 1.0), writes=["ones64"])
        S.op("pool", lambda e: e.memset(ones_row[:], 1.0), writes=["ones_row"])

        win_v = win_d.rearrange("(k p) c -> k p c", p=128)
        conv_eng = ["pool", "pool", "pool", "pool"]
        stage_i = [0]

        def stage_piece(pc):
            c0, c1 = PIECES[pc]
            for k in range(8):
                i = stage_i[0]
                stage_i[0] += 1
                si = i % 6
                if si < 4:
                    skey, ssem, src = ("x", 4 + si), d_x[4 + si], x_t[:, 4 + si, 0:c1 - c0]
                else:
                    skey, ssem, src = ("y", si - 4), d_y[si - 4], y_t[:, si - 4, 0:c1 - c0]
                S.op("sp", lambda e, k=k, src=src: e.dma_start(out=src, in_=win_v[k, :, c0:c1]),
                     writes=[skey], dsem=ssem)
                ce = conv_eng[i % 4]
                dst = w_in[:, k, c0:c1]
                if ce == "dve":
                    fn = lambda e, dst=dst, src=src, k=k: e.tensor_scalar(out=dst, in0=src, scalar1=gcol[:, k:k + 1], scalar2=None, op0=ALU.mult)
                elif ce == "act":
                    fn = lambda e, dst=dst, src=src, k=k: e.activation(out=dst, in_=src, func=AF.Copy, scale=gcol[:, k:k + 1])
                else:
                    fn = lambda e, dst=dst, src=src, k=k: e.tensor_scalar(out=dst, in0=src, scalar1=gcol[:, k:k + 1], scalar2=1.0, op0=ALU.mult, op1=ALU.mult)
                S.op(ce, fn, reads=[skey, "c_gcol"], writes=[("w_in", k, pc)])

        def load_w_out():
            wout_v = wout_d.rearrange("(k p) c -> k p c", p=128)
            for k in range(8):
                S.op("pool", lambda e, k=k: e.dma_start(out=w_out[:, k, :], in_=wout_v[k]), writes=[("w_out", k)], dsem=d_wo)
            S.regroup([("w_out", k) for k in range(8)])

        def setup_late():
            S.op("dve", lambda e: e.tensor_tensor(out=WT[:], in0=wt_stage, in1=tril[:].unsqueeze(1).to_broadcast([128, 8, 128]), op=ALU.mult),
                 reads=[("sgt", 0), ("sgt", 1), "c_tril"], writes=["WT"])
            S.op("act", lambda e: e.activation(out=esink_col[:], in_=esink_col[:], func=AF.Exp), reads=["c_sink"], writes=["esink_col"])
            S.op("dve", lambda e: e.tensor_copy(out=bvs_row[:], in_=bvs_f[:]), reads=["c_bvs"], writes=["bvs_row"])
            cb = newbank()

            def cmat_mm(e):
                ins = None
                for jj in range(4):
                    for hh in range(2):
                        ins = e.matmul(pbank[cb][hh * 64:(hh + 1) * 64, jj * 128:(jj + 1) * 128], lhsT=ones64[:, :],
                                       rhs=WT[:, 2 * jj + hh, :], start=True, stop=True)
                return ins
            S.op("pe", cmat_mm, reads=["ones64", "WT"], writes=[("pb", cb)])
            for jj in range(4):
                S.op("dve", lambda e, jj=jj: e.scalar_tensor_tensor(out=Cmat[:, jj, :], in0=pbank[cb][:, jj * 128:(jj + 1) * 128],
                                                                    scalar=lnb[:, jj:jj + 1], in1=bsB[:, jj, :], op0=ALU.mult, op1=ALU.add),
                     reads=[("pb", cb), "c_lnb", "c_bsB"], writes=[("Cmat", jj)])

        def wkeys(pc):
            return [("w_in", k, pc) for k in range(8)]

        def A_pre_units(st, use_pool=False, part=0):
            def mk(tt):
                def unit():
                    n = 4 * st + tt
                    slot = n % NX
                    xs = x_t[:, slot, :]
                    if part != 2:
                        S.op("act", lambda e: e.activation(out=junk[:], in_=xs, func=AF.Square, accum_out=st_ss[:, tt:tt + 1]),
                             reads=[("x", slot)], writes=["junk", ("ss", tt)])
                    if use_pool and part == 2:
                        S.op("act", lambda e: e.activation(out=xn[:, tt, :], in_=xs, func=AF.Copy, scale=st_rs[:, tt:tt + 1]),
                             reads=[("x", slot), ("rs", tt)], writes=[("xn", tt)])
                    elif use_pool:
                        S.op("pool", lambda e: e.tensor_scalar(out=st_ms[:, tt:tt + 1], in0=st_ss[:, tt:tt + 1], scalar1=1.0 / D, scalar2=EPS,
                                                               op0=ALU.mult, op1=ALU.add),
                             reads=[("ss", tt)], writes=[("ms", tt)])
                        S.op("pool", lambda e: e.tensor_tensor(out=st_rs[:, tt:tt + 1], in0=st_ms[:, tt:tt + 1], in1=nh[:], op=ALU.pow),
                             reads=[("ms", tt), "nh"], writes=[("rs", tt)])
                        if part == 0:
                            S.op("act", lambda e: e.activation(out=xn[:, tt, :], in_=xs, func=AF.Copy, scale=st_rs[:, tt:tt + 1]),
                                 reads=[("x", slot), ("rs", tt)], writes=[("xn", tt)])
                    else:
                        S.op("dve", lambda e: e.tensor_scalar(out=st_ms[:, tt:tt + 1], in0=st_ss[:, tt:tt + 1], scalar1=1.0 / D, scalar2=EPS,
                                                              op0=ALU.mult, op1=ALU.add),
                             reads=[("ss", tt)], writes=[("ms", tt)])
                        S.op("act", lambda e: e.activation(out=st_rs[:, tt:tt + 1], in_=st_ms[:, tt:tt + 1], func=AF.Ln),
                             reads=[("ms", tt)], writes=[("rs", tt)])
                        S.op("act", lambda e: e.activation(out=st_rs[:, tt:tt + 1], in_=st_rs[:, tt:tt + 1], func=AF.Exp, scale=-0.5),
                             reads=[("rs", tt)], writes=[("rs", tt)])
                        S.op("dve", lambda e: e.tensor_scalar(out=xn[:, tt, :], in0=xs, scalar1=st_rs[:, tt:tt + 1], scalar2=None, op0=ALU.mult),
                             reads=[("x", slot), ("rs", tt)], writes=[("xn", tt)])
                return unit
            return [mk(tt) for tt in range(4)]

        def XB_units(st):
            def mk(tt):
                def unit():
                    slot = (4 * st + tt) % NX
                    xs = x_t[:, slot, :]
                    S.op("pool", lambda e: e.tensor_tensor(out=xs, in0=xs, in1=bout[:], op=ALU.add),
                         reads=[("x", slot), "c_bout"], writes=[("x", slot)])
                return unit
            return [mk(tt) for tt in range(4)]

        def A_pe_units(st):
            def mk(pair):
                def unit():
                    tts = (2 * pair, 2 * pair + 1)
                    banks = [newbank(), newbank()]
                    pts = [pbank[b][:].bitcast(BF16).rearrange("p (c t) -> p c t", c=8) for b in banks]

                    def tr(e):
                        ins = None
                        for i, tt in enumerate(tts):
                            for c in range(8):
                                ins = e.transpose(pts[i][:, c, :], xn[:, tt, c * 128:(c + 1) * 128], ident[:])
                        return ins
                    S.op("pe", tr, reads=[("xn", tt) for tt in tts] + ["c_ident"], writes=[("pb", b) for b in banks])
                    for i, tt in enumerate(tts):
                        S.op("dve", lambda e, i=i, tt=tt: e.tensor_copy(out=xT[:, :, tt * 128:(tt + 1) * 128], in_=pts[i]),
                             reads=[("pb", banks[i])], writes=[("xT", tt)])
                return unit
            return [mk(0), mk(1)]

        xT_keys = [("xT", tt) for tt in range(4)]

        def fm_unit(fis, func, dsts, keys):
            def unit():
                banks = [newbank() for _ in fis]

                def mm(e):
                    ins = None
                    for fi, b in zip(fis, banks):
                        col = FM_COLS[fi][0]
                        for k in range(8):
                            ins = e.matmul(pbank[b][:], lhsT=w_in[:, k, col:col + 128], rhs=xT[:, k, :], start=(k == 0), stop=(k == 7))
                    return ins
                pcs = sorted({FM_COLS[fi][1] for fi in fis})
                S.op("pe", mm, reads=xT_keys + [kk for pc in pcs for kk in wkeys(pc)], writes=[("pb", b) for b in banks])
                for fi, b, dst, key in zip(fis, banks, dsts, keys):
                    if func == AF.Identity:
                        S.op("dve", lambda e, fi=fi, b=b, dst=dst: e.tensor_scalar(out=dst, in0=pbank[b][:], scalar1=bias_fm[:, fi:fi + 1],
                                                                                   scalar2=None, op0=ALU.add),
                             reads=[("pb", b), "c_biasfm"], writes=[key])
                    else:
                        S.op("act", lambda e, fi=fi, b=b, dst=dst: e.activation(out=dst, in_=pbank[b][:], func=func, bias=bias_fm[:, fi:fi + 1]),
                             reads=[("pb", b), "c_biasfm"], writes=[key])
            return unit

        def BQKV_units(st):
            qb = st % 2
            us = [fm_unit([c, c + 1], AF.Identity, [qT[:, qb, c, :], qT[:, qb, c + 1, :]], [("qT", qb, c), ("qT", qb, c + 1)]) for c in (0, 2)]
            kslot = (st % 2) * 4
            us.append(fm_unit([4], AF.Identity, [kT[:, kslot:kslot + 4, :].rearrange("p a b -> p (a b)")], [("kT", st % 2)]))

            def vunit():
                b = newbank()
                vslot = (st % 2) * 4

                def mmv(e):
                    ins = None
                    for tt in range(4):
                        for k in range(8):
                            ins = e.matmul(pbank[b][:, tt * 128:(tt + 1) * 128], lhsT=xT[:, k, tt * 128:(tt + 1) * 128],
                                           rhs=w_in[:, k, OFF_V:OFF_V + 128], start=(k == 0), stop=(k == 7))
                    return ins
                S.op("pe", mmv, reads=xT_keys + wkeys(0), writes=[("pb", b)])
                S.op("dve", lambda e: e.tensor_tensor(out=v_r[:, vslot:vslot + 4, :], in0=pbank[b][:].rearrange("p (a c) -> p a c", a=4),
                                                      in1=bv_bc[:].unsqueeze(1).to_broadcast([128, 4, 128]), op=ALU.add),
                     reads=[("pb", b), "c_bv"], writes=[("v", vslot + i) for i in range(4)])
            us.append(vunit)
            return us

        def Brest_units(st):
            zb = st % 2
            us = [fm_unit([5 + c, 6 + c], AF.Gelu, [uT[:, c, :], uT[:, c + 1, :]], [("uT", c), ("uT", c + 1)]) for c in (0, 2)]

            def mkvs(tt):
                def unit():
                    b = newbank()
                    vb = tt % 2

                    def mmvs(e):
                        for k in range(8):
                            e.matmul(pbank[b][:], lhsT=xT[:, k, tt * 128:(tt + 1) * 128], rhs=w_in[:, k, OFF_VS:OFF_VS + 512],
                                     start=(k == 0), stop=False)
                        return e.matmul(pbank[b][:], lhsT=ones_row[0:1, :], rhs=bvs_row[0:1, :], start=False, stop=True)
                    S.op("pe", mmvs, reads=[("xT", tt), "ones_row", "bvs_row"] + wkeys(1), writes=[("pb", b)])
                    S.op("act", lambda e: e.activation(out=vs_f[:, vb, :], in_=pbank[b][:], func=AF.Gelu),
                         reads=[("pb", b)], writes=[("vs_f", vb)])
                    S.op("dve", lambda e: e.bn_stats(out=st_bn[:, vb, :], in_=vs_f[:, vb, :]), reads=[("vs_f", vb)], writes=[("bn", vb)])
                    S.op("dve", lambda e: e.bn_aggr(out=st_mv[:, vb, :], in_=st_bn[:, vb, :]), reads=[("bn", vb)], writes=[("mv", vb)])
                    S.op("dve", lambda e: e.tensor_scalar(out=st_ve[:, vb:vb + 1], in0=st_mv[:, vb, 1:2], scalar1=EPS, scalar2=None, op0=ALU.add),
                         reads=[("mv", vb)], writes=[("ve", vb)])
                    S.op("pool", lambda e: e.tensor_tensor(out=st_lr[:, vb:vb + 1], in0=st_ve[:, vb:vb + 1], in1=nh[:], op=ALU.pow),
                         reads=[("ve", vb), "nh"], writes=[("lr", vb)])
                    S.op("dve", lambda e: e.tensor_scalar(out=vln[:, tt, :], in0=vs_f[:, vb, :], scalar1=st_mv[:, vb, 0:1],
                                                          scalar2=st_lr[:, vb:vb + 1], op0=ALU.subtract, op1=ALU.mult),
                         reads=[("vs_f", vb), ("mv", vb), ("lr", vb)], writes=[("vln", tt)])
                return unit
            us += [mkvs(tt) for tt in range(4)]
            us += [fm_unit([9 + c, 10 + c], AF.Silu, [zaT[:, zb, c, :], zaT[:, zb, c + 1, :]], [("zaT", zb, c), ("zaT", zb, c + 1)])
                   for c in (0, 2)]
            us += [fm_unit([13 + c, 14 + c], AF.Silu, [zsT[:, c, :], zsT[:, c + 1, :]], [("zsT", c), ("zsT", c + 1)]) for c in (0, 2)]
            return us

        cstate = {}

        def C1_unit(st, j):
            def unit():
                n = 4 * st + j
                qb = st % 2
                kbs = [1] if n == 0 else [0, 1]
                qkeys = [("qT", qb, c) for c in range(4)]
                sbk = {}
                order = [(g, kb) for kb in kbs for g in range(2)]
                for gk in order:
                    sbk[gk] = newbank()

                def mms(e):
                    ins = None
                    for (g, kb) in order:
                        kslot = (n - 1 + kb) % 8
                        ins = e.matmul(pbank[sbk[(g, kb)]][:], lhsT=kT[g * 64:(g + 1) * 64, kslot, :],
                                       rhs=qT[g * 64:(g + 1) * 64, qb, :, j * 128:(j + 1) * 128], start=True, stop=True)
                    return ins
                S.op("pe", mms, reads=qkeys + [("kT", ((n - 1 + kb) // 4) % 2) for kb in kbs], writes=[("pb", b) for b in sbk.values()])
                pp = n % 2
                for kb in kbs:
                    for g in range(2):
                        b = sbk[(g, kb)]
                        S.op("act", lambda e, g=g, kb=kb, b=b: e.activation(out=PT[:, pp, kb, g, :], in_=pbank[b][:], func=AF.Exp, scale=0.125),
                             reads=[("pb", b)], writes=[("PT", pp, kb, g)])
                        if kb == 0:
                            def msk(e, g=g, kb=kb):
                                v = PT[:, pp, kb, g, :].rearrange("p (c q) -> p c q", c=4)
                                return e.affine_select(out=v, in_=v, pattern=[[0, 4], [-1, 128]], compare_op=ALU.is_ge, fill=0.0,
                                                       base=-1, channel_multiplier=1)
                            S.op("pool", msk, reads=[("PT", pp, kb, g)], writes=[("PT", pp, kb, g)])
                        else:
                            S.op("dve", lambda e, g=g, kb=kb: e.tensor_tensor(out=PT[:, pp, kb, g, :], in0=PT[:, pp, kb, g, :],
                                                                              in1=mask[:, kb, :], op=ALU.mult),
                                 reads=[("PT", pp, kb, g), "c_mask"], writes=[("PT", pp, kb, g)])
            return unit

        def C2_unit(st, j):
            def unit():
                n = 4 * st + j
                zb = st % 2
                pp = n % 2
                kbs = [1] if n == 0 else [0, 1]
                bn_ = newbank()
                bd_ = newbank()

                def mmpv(e):
                    for g in range(2):
                        for i, kb in enumerate(kbs):
                            kn = n - 1 + kb
                            e.matmul(pbank[bn_][g * 64:(g + 1) * 64, :], lhsT=v_r[:, kn % 8, g * 64:(g + 1) * 64], rhs=PT[:, pp, kb, g, :],
                                     start=(i == 0), stop=(i == len(kbs) - 1))
                    ins = None
                    for g in range(2):
                        for i, kb in enumerate(kbs):
                            ins = e.matmul(pbank[bd_][g * 64:(g + 1) * 64, :], lhsT=ones64[:, :], rhs=PT[:, pp, kb, g, :],
                                           start=(i == 0), stop=(i == len(kbs) - 1))
                    return ins
                vkeys = [("v", (n - 1 + kb) % 8) for kb in kbs]
                S.op("pe", mmpv, reads=[("PT", pp, kb, g) for kb in kbs for g in range(2)] + vkeys + ["ones64"],
                     writes=[("pb", bn_), ("pb", bd_)])
                lb = j % 2
                def lnop(e):
                    ins = None
                    for c in range(4):
                        ins = e.activation(out=LR[:, lb, c * 128:(c + 1) * 128], in_=pbank[bd_][:, c * 128:(c + 1) * 128], func=AF.Ln,
                                           bias=esink_col[:, c:c + 1])
                    return ins
                S.op("act", lnop, reads=[("pb", bd_), "esink_col"], writes=[("LR", lb)])
                S.op("act", lambda e: e.activation(out=LR[:, lb, :], in_=LR[:, lb, :], func=AF.Exp, scale=-1.0), reads=[("LR", lb)], writes=[("LR", lb)])
                S.op("pool", lambda e: e.tensor_tensor(out=LR[:, lb, :].rearrange("p (c q) -> p c q", c=4),
                                                       in0=LR[:, lb, :].rearrange("p (c q) -> p c q", c=4),
                                                       in1=zaT[:, zb, :, j * 128:(j + 1) * 128], op=ALU.mult),
                     reads=[("LR", lb)] + [("zaT", zb, c) for c in range(4)], writes=[("LR", lb)])
                S.op("dve", lambda e: e.tensor_tensor(out=mixT[:, 0:4, j * 128:(j + 1) * 128],
                                                      in0=pbank[bn_][:].rearrange("p (c q) -> p c q", c=4),
                                                      in1=LR[:, lb, :].rearrange("p (c q) -> p c q", c=4), op=ALU.mult),
                     reads=[("pb", bn_), ("LR", lb)], writes=[("mixA", j)])
            return unit

        def D_units(st):
            def unit():
                banks = [newbank() for _ in range(4)]

                def mmg(e):
                    ins = None
                    for jj in range(4):
                        for pc in range(4):
                            for hh in range(2):
                                h = 2 * jj + hh
                                ins = e.matmul(pbank[banks[jj]][hh * 64:(hh + 1) * 64, pc * 128:(pc + 1) * 128],
                                               lhsT=vln[:, pc, h * 64:(h + 1) * 64], rhs=WT[:, h, :], start=True, stop=True)
                    return ins
                S.op("pe", mmg, reads=[("vln", pc) for pc in range(4)] + ["WT"], writes=[("pb", b) for b in banks])
                for jj in range(4):
                    b = banks[jj]
                    gb = jj
                    S.op("pool", lambda e, jj=jj, gb=gb: e.tensor_tensor(out=gate[:, gb, :], in0=uT[:, jj, :], in1=zsT[:, jj, :], op=ALU.mult),
                         reads=[("uT", jj), ("zsT", jj)], writes=[("gate", gb)])
                    S.op("dve", lambda e, jj=jj, gb=gb, b=b: e.scalar_tensor_tensor(
                        out=sgt[:, gb, :].rearrange("p (c t) -> p c t", c=4), in0=pbank[b][:].rearrange("p (c t) -> p c t", c=4),
                        scalar=lng[:, jj:jj + 1], in1=Cmat[:, jj, :].unsqueeze(1).to_broadcast([128, 4, 128]), op0=ALU.mult, op1=ALU.add),
                        reads=[("pb", b), "c_lng", ("Cmat", jj)], writes=[("sgt", gb)])
                for jj in range(4):
                    gb = jj
                    S.op("dve", lambda e, jj=jj, gb=gb: e.tensor_tensor(out=mixT[:, 4 + jj, :], in0=sgt[:, gb, :], in1=gate[:, gb, :], op=ALU.mult),
                         reads=[("sgt", gb), ("gate", gb)], writes=[("mixS", jj)])
            return [unit]

        def E_units(st):
            wo_keys = [("w_out", k) for k in range(8)]

            def mk(tt):
                def unit():
                    n = 4 * st + tt
                    slot = n % NX
                    ys = n % 2
                    mkeys = [("mixA", tt)] + [("mixS", jj) for jj in range(4)]
                    banks = [newbank(), newbank()]

                    def mmo(e):
                        ins = None
                        for hf in range(2):
                            for ec in range(8):
                                ins = e.matmul(pbank[banks[hf]][:], lhsT=mixT[:, ec, tt * 128:(tt + 1) * 128],
                                               rhs=w_out[:, ec, hf * 512:(hf + 1) * 512], start=(ec == 0), stop=(ec == 7))
                        return ins
                    S.op("pe", mmo, reads=mkeys + wo_keys, writes=[("pb", banks[0]), ("pb", banks[1])])

                    def resid(e):
                        ins = None
                        for hf in range(2):
                            ins = e.tensor_tensor(out=y_t[:, ys, hf * 512:(hf + 1) * 512], in0=pbank[banks[hf]][:],
                                                  in1=x_t[:, slot, hf * 512:(hf + 1) * 512], op=ALU.add)
                        return ins
                    S.op("dve", resid, reads=[("pb", banks[0]), ("pb", banks[1]), ("x", slot)], writes=[("y", ys)])
                    if n + NX < NT:
                        load_x(n + NX)
                    S.op("act", lambda e: e.activation(out=junk[:], in_=y_t[:, ys, :], func=AF.Square, accum_out=st_fs[:, ys:ys + 1]),
                         reads=[("y", ys)], writes=["junk", ("fs", ys)])
                    S.op("dve", lambda e: e.tensor_scalar(out=st_fm[:, ys:ys + 1], in0=st_fs[:, ys:ys + 1], scalar1=1.0 / D, scalar2=EPS,
                                                          op0=ALU.mult, op1=ALU.add),
                         reads=[("fs", ys)], writes=[("fm", ys)])
                    S.op("act", lambda e: e.activation(out=st_fr[:, ys:ys + 1], in_=st_fm[:, ys:ys + 1], func=AF.Ln),
                         reads=[("fm", ys)], writes=[("fr", ys)])
                    S.op("act", lambda e: e.activation(out=st_fr[:, ys:ys + 1], in_=st_fr[:, ys:ys + 1], func=AF.Exp, scale=-0.5),
                         reads=[("fr", ys)], writes=[("fr", ys)])
                    S.op("dve", lambda e: e.scalar_tensor_tensor(out=y_t[:, ys, :], in0=y_t[:, ys, :], scalar=st_fr[:, ys:ys + 1], in1=gf[:],
                                                                 op0=ALU.mult, op1=ALU.mult),
                         reads=[("y", ys), ("fr", ys), "c_gf"], writes=[("y", ys)])
                    S.op("sp", lambda e: e.dma_start(out=out_d[n * 128:(n + 1) * 128, :], in_=y_t[:, ys, :]),
                         reads=[("y", ys)], writes=[("y", ys), ("out", n)], dsem=d_y[ys])
                return unit
            return [mk(tt) for tt in range(4)]

        def run(units):
            for u in units:
                u()

        run(A_pre_units(0))
        run(A_pe_units(0))
        stage_piece(0)
        load_consts_mid()
        stage_piece(1)
        stage_piece(2)
        load_consts_late()
        for n in range(4, 8):
            load_x(n)
        load_w_out()
        run(BQKV_units(0))
        setup_late()
        br0 = Brest_units(0)
        run(br0[:6])
        run(A_pre_units(1, use_pool=True, part=1))
        warm(AF.Silu)
        run(br0[8:10])
        run(D_units(0))
        run(A_pre_units(1, use_pool=True, part=2))
        run(br0[6:8])
        warm(AF.Exp)
        run(XB_units(0))
        for st in range(NS):
            nxt = st + 1 < NS
            F = (A_pe_units(st + 1) + BQKV_units(st + 1)) if nxt else []
            fi = [0]

            def f(k):
                for _ in range(k):
                    if fi[0] < len(F):
                        F[fi[0]]()
                        fi[0] += 1
            Eu = E_units(st)
            f(1)
            C1_unit(st, 0)()
            f(1)
            C1_unit(st, 1)()
            C2_unit(st, 0)()
            f(1)
            C1_unit(st, 2)()
            C2_unit(st, 1)()
            Eu[0]()
            f(1)
            C1_unit(st, 3)()
            C2_unit(st, 2)()
            Eu[1]()
            f(1)
            C2_unit(st, 3)()
            Eu[2]()
            f(100)
            Eu[3]()
            if nxt:
                warm(AF.Gelu)
                br = Brest_units(st + 1)
                run(XB_units(st + 1))
                run(br[:6])
                if st + 2 < NS:
                    run(A_pre_units(st + 2, use_pool=True, part=1))
                warm(AF.Silu)
                run(br[8:10])
                run(D_units(st + 1))
                if st + 2 < NS:
                    run(A_pre_units(st + 2, use_pool=True, part=2))
                run(br[6:8])
                warm(AF.Exp)

        S.final_wait("sp", [("out", n) for n in range(NT)])
        with nc.Block() as block:
            S.emit(block)
    return nc


def _host_layout(inp):
    w_in = np.asarray(inp["w_in"])[0]
    b_in = np.asarray(inp["b_in"])[0]
    w_out = np.asarray(inp["w_out"])[0]
    hp = np.concatenate([np.r_[c * 64:(c + 1) * 64, (c + 4) * 64:(c + 5) * 64] for c in range(4)])
    cols = np.concatenate([hp, np.arange(512, 640), np.arange(640, 768), np.arange(1280, 1792), np.arange(1792, 2304),
                           768 + hp, np.arange(2304, 2816)])
    w_in_p = np.ascontiguousarray(w_in[:, cols])
    b_in_p = b_in[cols]
    rows = np.concatenate([hp, np.arange(512, 1024)])
    w_out_p = np.ascontiguousarray(w_out[rows, :])
    f32 = np.float32
    shared = {
        "w_in": w_in_p,
        "w_out": w_out_p,
        "gcol": np.ascontiguousarray(np.asarray(inp["norm_g"])[0].reshape(8, 128).T),
        "bias_fm": np.ascontiguousarray(np.stack([b_in_p[c0:c0 + 128] for (c0, _) in FM_COLS], axis=1)),
        "bv_bc": np.ascontiguousarray(np.broadcast_to(b_in_p[OFF_V:OFF_V + 128], (128, 128))),
        "bvs_row": np.ascontiguousarray(b_in_p[OFF_VS:OFF_VS + 512].reshape(1, 512)),
        "sinks": np.ascontiguousarray(np.repeat(np.asarray(inp["attn_sinks"])[0].reshape(2, 1, 4), 64, axis=1).reshape(128, 4)),
        "sgu_wT": np.ascontiguousarray(np.asarray(inp["sgu_w"])[0].transpose(2, 0, 1)),
        "tril": np.triu(np.ones((128, 128), f32)),
        "bsB": np.ascontiguousarray(np.repeat(np.asarray(inp["sgu_b"])[0].reshape(4, 2, 1, 128), 64, axis=2)
                                    .reshape(4, 128, 128).transpose(1, 0, 2)),
        "lnb_col": np.ascontiguousarray(np.asarray(inp["sgu_ln_b"])[0].reshape(4, 128).T),
        "lng_col": np.ascontiguousarray(np.asarray(inp["sgu_ln_g"])[0].reshape(4, 128).T),
        "bout_bc": np.ascontiguousarray(np.broadcast_to(np.asarray(inp["b_out"])[0], (128, D))),
        "gf_bc": np.ascontiguousarray(np.broadcast_to(np.asarray(inp["final_norm_g"]), (128, D))),
        "ident": np.eye(128, dtype=f32).astype(ml_dtypes.bfloat16),
    }
    mC = np.triu(np.ones((128, 128), f32))
    mP = 1.0 - mC
    m = np.stack([np.tile(mP, (1, 4)), np.tile(mC, (1, 4))], axis=1)
    shared["mask"] = np.ascontiguousarray(m).astype(ml_dtypes.bfloat16)
    return {k: (v if v.dtype == ml_dtypes.bfloat16 else np.ascontiguousarray(v, dtype=f32)) for k, v in shared.items()}


_NC_CACHE = {}


def kernel(**inputs):
    x = np.asarray(inputs["x"], dtype=np.float32)
    shared = _host_layout(inputs)
    if "nc" not in _NC_CACHE:
        _NC_CACHE["nc"] = build_program()
    nc = _NC_CACHE["nc"]
    in_maps = []
    for c in range(NCORES):
        m = dict(shared)
        m["x"] = np.ascontiguousarray(x[c])
        in_maps.append(m)
    res = run_bass_kernel_spmd(nc, in_maps, core_ids=list(range(NCORES)))
    out = np.stack([np.asarray(res.results[c]["out"], dtype=np.float32) for c in range(NCORES)], axis=0)
    return out
```
